# Optimizing a Trainium2 kernel written in Bass

```python
import math
import jax
import jax.numpy as jnp
from jax import lax
import numpy as np

D_MODEL = 1024
BATCH = 4
SEQ = 8192
DEPTH = 4

HEAD_DIM = 64
N_HEADS = D_MODEL // HEAD_DIM
D_INNER = N_HEADS * HEAD_DIM
N_MIXERS = 4
ROPE_THETA = 500000.0
ROT_DIM = HEAD_DIM // 4
Q_BLOCK = 128
LN_EPS = 1e-5
DN_ALPHA = (2.0 * DEPTH) ** 0.25
DN_BETA = (8.0 * DEPTH) ** -0.25
MOBA_BLOCK = 256
MOBA_TOPK = 3
MOBA_Q_CHUNK = 16
SWA_WINDOW = 128
SWA_KV_HEADS = 4
DILATED_GROUPS = ((128, 1), (512, 4), (2048, 16))

SB_IN = 4 * D_INNER
MOBA_IN = 4 * D_INNER
SWA_IN = 2 * D_INNER + 2 * SWA_KV_HEADS * HEAD_DIM
DIL_IN = (3 * len(DILATED_GROUPS) + 1) * D_INNER

kernel_name = 'hybrid_sb_moba_swa_dilated_deepnorm'


def layer_norm(x, g, b):
    xf = x.astype(jnp.float32)
    mu = jnp.mean(xf, axis=-1, keepdims=True)
    var = jnp.mean(jnp.square(xf - mu), axis=-1, keepdims=True)
    return ((xf - mu) * lax.rsqrt(var + LN_EPS) * g + b).astype(x.dtype)


def split_cols(p, sizes):
    idx = [int(c) for c in np.cumsum(sizes)[:-1]]
    return jnp.split(p, idx, axis=-1)


def rope_tables(seq_len):
    pos = jnp.arange(seq_len, dtype=jnp.float32)
    inv = ROPE_THETA ** (-jnp.arange(0, ROT_DIM, 2, dtype=jnp.float32) / ROT_DIM)
    ang = pos[:, None] * inv[None, :]
    return jnp.cos(ang)[:, None, :], jnp.sin(ang)[:, None, :]


def partial_rope(x, cos, sin):
    half = ROT_DIM // 2
    x1 = x[..., :half].astype(jnp.float32)
    x2 = x[..., half:ROT_DIM].astype(jnp.float32)
    r1 = (x1 * cos - x2 * sin).astype(x.dtype)
    r2 = (x2 * cos + x1 * sin).astype(x.dtype)
    return jnp.concatenate([r1, r2, x[..., ROT_DIM:]], axis=-1)


def stick_breaking_attn(q, k, v):
    B, S, H, Dh = q.shape
    nb = S // Q_BLOCK
    scale = Dh ** -0.5
    qb = q.reshape(B, nb, Q_BLOCK, H, Dh).transpose(1, 0, 3, 2, 4)
    kt = k.transpose(0, 2, 1, 3)
    vt = v.transpose(0, 2, 1, 3)
    kpos = jnp.arange(S)

    def one_block(args):
        qi, bi = args
        tpos = bi * Q_BLOCK + jnp.arange(Q_BLOCK)
        before = kpos[None, :] < tpos[:, None]
        z = jnp.einsum('bhqd,bhsd->bhqs', qi, kt).astype(jnp.float32) * scale
        log_1m = jnp.where(before, jax.nn.log_sigmoid(-z), 0.0)
        between = lax.cumsum(log_1m, axis=3, reverse=True) - log_1m
        log_a = jnp.where(before, jax.nn.log_sigmoid(z) + between, -jnp.inf)
        return jnp.einsum('bhqs,bhsd->bhqd', jnp.exp(log_a).astype(vt.dtype), vt)

    out = lax.map(one_block, (qb, jnp.arange(nb)))
    return out.transpose(1, 0, 3, 2, 4).reshape(B, S, H, Dh)


def moba_attn(q, k, v):
    B, S, H, Dh = q.shape
    scale = Dh ** -0.5
    nkb = -(-S // MOBA_BLOCK)
    pad = nkb * MOBA_BLOCK - S
    qt = q.transpose(0, 2, 1, 3)
    kp = jnp.pad(k.transpose(0, 2, 1, 3), ((0, 0), (0, 0), (0, pad), (0, 0)))
    vp = jnp.pad(v.transpose(0, 2, 1, 3), ((0, 0), (0, 0), (0, pad), (0, 0)))
    kblk = kp.reshape(B, H, nkb, MOBA_BLOCK, Dh)
    vblk = vp.reshape(B, H, nkb, MOBA_BLOCK, Dh)
    k_mean = jnp.mean(kblk.astype(jnp.float32), axis=3)
    gate = jnp.einsum('bhsd,bhnd->bhsn', qt.astype(jnp.float32), k_mean)
    q_blk = jnp.arange(S) // MOBA_BLOCK
    fully_past = jnp.arange(nkb)[None, :] < q_blk[:, None]
    gate = jnp.where(fully_past, gate, -jnp.inf)
    topk = min(MOBA_TOPK, nkb)
    _, sel = lax.top_k(gate, topk)
    sel_ok = sel < q_blk[:, None]

    C = MOBA_Q_CHUNK
    nc = S // C
    n_sel = topk * MOBA_BLOCK
    q_c = qt.reshape(B, H, nc, C, Dh).transpose(2, 0, 1, 3, 4)
    sel_c = sel.reshape(B, H, nc, C, topk).transpose(2, 0, 1, 3, 4)
    ok_c = sel_ok.reshape(B, H, nc, C, topk).transpose(2, 0, 1, 3, 4)
    b_idx = jnp.arange(B)[:, None, None, None]
    h_idx = jnp.arange(H)[None, :, None, None]

    def one_chunk(args):
        qc, selc, okc, ci = args
        t = ci * C + jnp.arange(C)
        k_sel = kblk[b_idx, h_idx, selc]
        v_sel = vblk[b_idx, h_idx, selc]
        own = (ci * C) // MOBA_BLOCK * MOBA_BLOCK
        k_own = lax.dynamic_slice_in_dim(kp, own, MOBA_BLOCK, axis=2)
        v_own = lax.dynamic_slice_in_dim(vp, own, MOBA_BLOCK, axis=2)
        s_sel = jnp.einsum('bhqd,bhqnkd->bhqnk', qc, k_sel).astype(jnp.float32) * scale
        s_sel = jnp.where(okc[..., None], s_sel, -jnp.inf).reshape(B, H, C, n_sel)
        s_own = jnp.einsum('bhqd,bhkd->bhqk', qc, k_own).astype(jnp.float32) * scale
        s_own = jnp.where((own + jnp.arange(MOBA_BLOCK))[None, :] <= t[:, None], s_own, -jnp.inf)
        p = jax.nn.softmax(jnp.concatenate([s_sel, s_own], axis=-1), axis=-1).astype(v.dtype)
        o = jnp.einsum('bhqnk,bhqnkd->bhqd', p[..., :n_sel].reshape(B, H, C, topk, MOBA_BLOCK), v_sel)
        return o + jnp.einsum('bhqk,bhkd->bhqd', p[..., n_sel:], v_own)

    out = lax.map(one_chunk, (q_c, sel_c, ok_c, jnp.arange(nc)))
    return out.transpose(1, 0, 3, 2, 4).reshape(B, S, H, Dh)


def banded_attn(q, k, v, max_back, sink_logits=None):
    N, L, H, Dh = q.shape
    G = k.shape[2]
    R = H // G
    scale = Dh ** -0.5
    nb = -(-L // Q_BLOCK)
    Lp = nb * Q_BLOCK
    n_prev = -(-max_back // Q_BLOCK)
    span = (n_prev + 1) * Q_BLOCK
    qb = jnp.pad(q, ((0, 0), (0, Lp - L), (0, 0), (0, 0))).reshape(N, nb, Q_BLOCK, G, R, Dh)
    kv_pad = ((0, 0), (n_prev * Q_BLOCK, Lp - L), (0, 0), (0, 0))
    kp = jnp.pad(k, kv_pad)
    vp = jnp.pad(v, kv_pad)

    def band(t):
        return jnp.concatenate(
            [t[:, j * Q_BLOCK:j * Q_BLOCK + Lp].reshape(N, nb, Q_BLOCK, G, Dh) for j in range(n_prev + 1)],
            axis=2)

    kb = band(kp)
    vb = band(vp)
    s = jnp.einsum('nbqgrd,nbkgd->nbgrqk', qb, kb).astype(jnp.float32) * scale
    rel = jnp.arange(span) - n_prev * Q_BLOCK
    dist = jnp.arange(Q_BLOCK)[:, None] - rel[None, :]
    key_pos = (jnp.arange(nb) * Q_BLOCK)[:, None] + rel[None, :]
    valid = ((dist >= 0) & (dist <= max_back))[None] & (key_pos >= 0)[:, None, :]
    s = jnp.where(valid[None, :, None, None], s, -jnp.inf)
    m = jnp.max(s, axis=-1, keepdims=True)
    if sink_logits is not None:
        sink = sink_logits.astype(jnp.float32).reshape(1, 1, G, R, 1, 1)
        m = jnp.maximum(m, sink)
    e = jnp.exp(s - m)
    den = jnp.sum(e, axis=-1, keepdims=True)
    if sink_logits is not None:
        den = den + jnp.exp(sink - m)
    o = jnp.einsum('nbgrqk,nbkgd->nbqgrd', (e / den).astype(v.dtype), vb)
    lse = (m + jnp.log(den))[..., 0]
    o = o.reshape(N, Lp, H, Dh)[:, :L]
    lse = lse.transpose(0, 1, 4, 2, 3).reshape(N, Lp, H)[:, :L]
    return o, lse


def dilated_attn(q, k, v, window, dilation):
    B, S, H, Dh = q.shape
    L = S // dilation

    def to_streams(t):
        return t.reshape(B, L, dilation, H, Dh).transpose(0, 2, 1, 3, 4).reshape(B * dilation, L, H, Dh)

    o, lse = banded_attn(to_streams(q), to_streams(k), to_streams(v), window // dilation)
    o = o.reshape(B, dilation, L, H, Dh).transpose(0, 2, 1, 3, 4).reshape(B, S, H, Dh)
    lse = lse.reshape(B, dilation, L, H).transpose(0, 2, 1, 3).reshape(B, S, H)
    return o, lse


def stick_breaking_branch(h, w_in):
    B, S, _ = h.shape
    q, k, v, z = split_cols(h @ w_in, (D_INNER, D_INNER, D_INNER, D_INNER))
    shp = (B, S, N_HEADS, HEAD_DIM)
    y = stick_breaking_attn(q.reshape(shp), k.reshape(shp), v.reshape(shp))
    return y.reshape(B, S, D_INNER), z


def moba_branch(h, w_in, cos, sin):
    B, S, _ = h.shape
    q, k, v, z = split_cols(h @ w_in, (D_INNER, D_INNER, D_INNER, D_INNER))
    shp = (B, S, N_HEADS, HEAD_DIM)
    q = partial_rope(q.reshape(shp), cos, sin)
    k = partial_rope(k.reshape(shp), cos, sin)
    y = moba_attn(q, k, v.reshape(shp))
    return y.reshape(B, S, D_INNER), z


def swa_branch(h, w_in, sinks, cos, sin):
    B, S, _ = h.shape
    kvw = SWA_KV_HEADS * HEAD_DIM
    q, k, v, z = split_cols(h @ w_in, (D_INNER, kvw, kvw, D_INNER))
    q = partial_rope(q.reshape(B, S, N_HEADS, HEAD_DIM), cos, sin)
    k = partial_rope(k.reshape(B, S, SWA_KV_HEADS, HEAD_DIM), cos, sin)
    v = v.reshape(B, S, SWA_KV_HEADS, HEAD_DIM)
    y, _ = banded_attn(q, k, v, SWA_WINDOW - 1, sinks)
    return y.reshape(B, S, D_INNER), z


def dilated_branch(h, w_in, cos, sin):
    B, S, _ = h.shape
    n_g = len(DILATED_GROUPS)
    parts = split_cols(h @ w_in, (D_INNER,) * (3 * n_g + 1))
    z = parts[-1]
    shp = (B, S, N_HEADS, HEAD_DIM)
    outs = []
    lses = []
    for g, (window, dil) in enumerate(DILATED_GROUPS):
        q = partial_rope(parts[3 * g].reshape(shp), cos, sin)
        k = partial_rope(parts[3 * g + 1].reshape(shp), cos, sin)
        v = parts[3 * g + 2].reshape(shp)
        o, lse = dilated_attn(q, k, v, window, dil)
        outs.append(o)
        lses.append(lse)
    wts = jax.nn.softmax(jnp.stack(lses), axis=0)
    o_all = jnp.stack(outs)
    y = jnp.einsum('gbsh,gbshd->bshd', wts.astype(o_all.dtype), o_all)
    return y.reshape(B, S, D_INNER), z


def setup_inputs(seed: int = 0) -> dict:
    key = jax.random.key(seed)
    ks = jax.random.split(key, 20)

    def w(k, fan_in, fan_out, scale=1.0):
        return jax.random.normal(k, (fan_in, fan_out), jnp.float32) * (scale * fan_in ** -0.5)

    def gain(k):
        return 1.0 + 0.02 * jax.random.normal(k, (D_MODEL,), jnp.float32)

    def bias(k):
        return 0.02 * jax.random.normal(k, (D_MODEL,), jnp.float32)

    return {
        'x': jax.random.normal(ks[0], (BATCH, SEQ, D_MODEL), jnp.float32),
        'sb_w_in': w(ks[1], D_MODEL, SB_IN),
        'sb_w_out': w(ks[2], D_INNER, D_MODEL, DN_BETA),
        'ln0_g': gain(ks[3]),
        'ln0_b': bias(ks[4]),
        'moba_w_in': w(ks[5], D_MODEL, MOBA_IN),
        'moba_w_out': w(ks[6], D_INNER, D_MODEL, DN_BETA),
        'ln1_g': gain(ks[7]),
        'ln1_b': bias(ks[8]),
        'swa_w_in': w(ks[9], D_MODEL, SWA_IN),
        'swa_sinks': 0.5 * jax.random.normal(ks[10], (N_HEADS,), jnp.float32),
        'swa_w_out': w(ks[11], D_INNER, D_MODEL, DN_BETA),
        'ln2_g': gain(ks[12]),
        'ln2_b': bias(ks[13]),
        'dil_w_in': w(ks[14], D_MODEL, DIL_IN),
        'dil_w_out': w(ks[15], D_INNER, D_MODEL, DN_BETA),
        'ln3_g': gain(ks[16]),
        'ln3_b': bias(ks[17]),
    }


def reference(x, sb_w_in, sb_w_out, ln0_g, ln0_b, moba_w_in, moba_w_out, ln1_g, ln1_b,
              swa_w_in, swa_sinks, swa_w_out, ln2_g, ln2_b, dil_w_in, dil_w_out, ln3_g, ln3_b):
    S = x.shape[1]
    cos, sin = rope_tables(S)
    mixers = (
        lambda h: stick_breaking_branch(h, sb_w_in),
        lambda h: moba_branch(h, moba_w_in, cos, sin),
        lambda h: swa_branch(h, swa_w_in, swa_sinks, cos, sin),
        lambda h: dilated_branch(h, dil_w_in, cos, sin),
    )
    w_outs = (sb_w_out, moba_w_out, swa_w_out, dil_w_out)
    ln_gs = (ln0_g, ln1_g, ln2_g, ln3_g)
    ln_bs = (ln0_b, ln1_b, ln2_b, ln3_b)
    for i in range(DEPTH):
        m = i % N_MIXERS
        y, z = mixers[m](x)
        out = (y * jax.nn.silu(z)) @ w_outs[m]
        x = layer_norm(DN_ALPHA * x + out, ln_gs[m], ln_bs[m])
    return x
```

```python
import math
from contextlib import ExitStack

import numpy as np
import concourse.bass as bass
import concourse.mybir as mybir
from concourse.bass_utils import run_bass_kernel_spmd

F32 = mybir.dt.float32
BF16 = mybir.dt.bfloat16
AF = mybir.ActivationFunctionType
ALU = mybir.AluOpType
AX = mybir.AxisListType

D = 1024
NH = 16
DH = 64
DEPTH = 4
NCORES = 8
ALPHA = (2.0 * DEPTH) ** 0.25
EPS = 1e-5
BIG = 30000.0
ROPE_THETA = 500000.0
DIL = ((128, 1), (512, 4), (2048, 16))

C_I, C_PM, C_SBM, C_CAUS, C_SWAP, C_DILP, C_ESEL = 0, 128, 256, 384, 512, 640, 768
NCONS = 768 + 32 * 128

ARENA_BYTES = 176 * 1024


def esz(dt):
    return 4 if dt == F32 else 2


class Sched:
    ENGS = ("pe", "act", "dve", "pool", "sp")

    def __init__(self, nc):
        self.nc = nc
        self.ops = []
        self.lastw = {}
        self.readers = {}
        self.dw = {}
        self.dma_last = {}
        self.eng_last = {}
        self.bar = {e: set() for e in self.ENGS}

    def op(self, eng, fn, r=(), w=(), dsem=None):
        idx = len(self.ops)
        deps = set()
        for k in r:
            if isinstance(k, str) and k.startswith("D:"):
                deps.update(self.dw.get(k, {}).values())
            else:
                lw = self.lastw.get(k)
                if lw is not None:
                    deps.add(lw)
        for k in w:
            if isinstance(k, str) and k.startswith("D:"):
                continue
            lw = self.lastw.get(k)
            if lw is not None:
                deps.add(lw)
            deps.update(self.readers.get(k, ()))
        if self.bar[eng]:
            deps.update(self.bar[eng])
            self.bar[eng] = set()
        for k in r:
            if not (isinstance(k, str) and k.startswith("D:")):
                self.readers.setdefault(k, []).append(idx)
        for k in w:
            if isinstance(k, str) and k.startswith("D:"):
                self.dw.setdefault(k, {})[dsem] = idx
            else:
                self.lastw[k] = idx
                self.readers[k] = []
        self.ops.append(dict(eng=eng, fn=fn, deps=deps, dsem=dsem))
        if dsem is not None:
            self.dma_last[dsem] = idx
        else:
            self.eng_last[eng] = idx
        return idx

    def barrier(self):
        pts = set(self.eng_last.values()) | set(self.dma_last.values())
        for e in self.ENGS:
            self.bar[e] = set(pts)

    def finish(self, final_eng="sp"):
        self.barrier()
        self.op(final_eng, lambda e: e.nop(), r=(), w=())

    def emit(self):
        nc = self.nc
        ops = self.ops
        n = len(ops)
        signal = [False] * n
        for i, o in enumerate(ops):
            for j in o["deps"]:
                oj = ops[j]
                if oj["dsem"] is not None:
                    continue
                if oj["eng"] == o["eng"] and o["eng"] in ("pe", "sp") and o["dsem"] is None:
                    continue
                signal[j] = True
        cnt = {e: 0 for e in self.ENGS}
        dcnt = {}
        tok = [None] * n
        for i, o in enumerate(ops):
            if o["dsem"] is not None:
                dcnt[o["dsem"]] = dcnt.get(o["dsem"], 0) + 16
                tok[i] = (("d", o["dsem"]), dcnt[o["dsem"]])
            elif signal[i]:
                cnt[o["eng"]] += 1
                tok[i] = (("e", o["eng"]), cnt[o["eng"]])
        self.stats = dict(n=n, cnt=dict(cnt), ndsem=len(dcnt))
        with ExitStack() as st:
            sems = {}
            for e in self.ENGS:
                sems[("e", e)] = st.enter_context(nc.semaphore("s_" + e))
            for k in dcnt:
                sems[("d", k)] = st.enter_context(nc.semaphore("d_" + str(k).replace(":", "_")))
            per = {e: [] for e in self.ENGS}
            for i, o in enumerate(ops):
                per[o["eng"]].append(i)
            block = st.enter_context(nc.Block())

            def run(ename, e):
                waited = {}
                for i in per[ename]:
                    o = ops[i]
                    need = {}
                    for j in o["deps"]:
                        t = tok[j]
                        if t is None:
                            continue
                        oj = ops[j]
                        if oj["dsem"] is None and oj["eng"] == ename and ename in ("pe", "sp") and o["dsem"] is None:
                            continue
                        if t[1] > need.get(t[0], 0):
                            need[t[0]] = t[1]
                    for sk, v in need.items():
                        if waited.get(sk, 0) >= v:
                            continue
                        waited[sk] = v
                        e.wait_ge(sems[sk], v)
                    ins = o["fn"](e)
                    t = tok[i]
                    if t is not None:
                        ins.then_inc(sems[t[0]], 16 if o["dsem"] is not None else 1)

            @block.tensor
            def _(e):
                run("pe", e)

            @block.scalar
            def _(e):
                run("act", e)

            @block.vector
            def _(e):
                run("dve", e)

            @block.gpsimd
            def _(e):
                run("pool", e)

            @block.sync
            def _(e):
                run("sp", e)


def make_consts():
    c = np.zeros((128, NCONS), np.float32)
    p = np.arange(128)[:, None]
    q = np.arange(128)[None, :]
    c[:, C_I:C_I + 128] = (p == q)
    pm = np.zeros((128, 128), np.float32)
    for o in (0, 64):
        for d in range(8):
            pm[o + d + 8, o + d] = -1.0
            pm[o + d, o + d + 8] = 1.0
    c[:, C_PM:C_PM + 128] = pm
    c[:, C_SBM:C_SBM + 128] = np.where(q <= p, -BIG, 0.0)
    c[:, C_CAUS:C_CAUS + 128] = np.where(p <= q, 0.0, -BIG)
    c[:, C_SWAP:C_SWAP + 128] = np.where(p > q, 0.0, -BIG)
    c[:, C_DILP:C_DILP + 128] = np.where(p >= q, 0.0, -BIG)
    for n in range(32):
        c[n, C_ESEL + n * 128:C_ESEL + (n + 1) * 128] = 1.0
    return c


def rope_tabs(S):
    pos = np.arange(S, dtype=np.float32)
    inv = (np.float32(ROPE_THETA) ** (-np.arange(0, 16, 2, dtype=np.float32) / np.float32(16))).astype(np.float32)
    ang = (pos[:, None] * inv[None, :]).astype(np.float32)
    cs = np.cos(ang).astype(np.float32).T
    sn = np.sin(ang).astype(np.float32).T
    cos = np.ones((128, S), np.float32)
    sin = np.zeros((128, S), np.float32)
    for o in (0, 64):
        cos[o:o + 8] = cs
        cos[o + 8:o + 16] = cs
        sin[o:o + 8] = sn
        sin[o + 8:o + 16] = sn
    return cos, sin


def layer_cols(layer, hh):
    fm = []
    vg = []
    h0 = hh * 512
    ar = np.arange
    if layer in (0, 1):
        rope = layer == 1
        for t in range(4):
            fm.append(("q", ar(h0 + t * 128, h0 + t * 128 + 128), rope))
        for t in range(4):
            fm.append(("k", 1024 + ar(h0 + t * 128, h0 + t * 128 + 128), rope))
        for t in range(4):
            fm.append(("z", 3072 + ar(h0 + t * 128, h0 + t * 128 + 128), False))
        vg.append((2048 + ar(h0, h0 + 512), [(h, h % 2) for h in range(8)]))
    elif layer == 2:
        for t in range(4):
            fm.append(("q", ar(h0 + t * 128, h0 + t * 128 + 128), True))
        for g in range(2):
            kv = 2 * hh + g
            cc = 1024 + ar(kv * 64, kv * 64 + 64)
            fm.append(("k", np.concatenate([cc, cc]), True))
        for t in range(4):
            fm.append(("z", 1536 + ar(h0 + t * 128, h0 + t * 128 + 128), False))
        vg.append((1280 + ar(2 * hh * 64, 2 * hh * 64 + 128), [(0, 0), (0, 1), (1, 0), (1, 1)]))
    else:
        for g in range(3):
            for t in range(4):
                fm.append(("q", (3 * g) * 1024 + ar(h0 + t * 128, h0 + t * 128 + 128), True))
            for t in range(4):
                fm.append(("k", (3 * g + 1) * 1024 + ar(h0 + t * 128, h0 + t * 128 + 128), True))
        for t in range(4):
            fm.append(("z", 9216 + ar(h0 + t * 128, h0 + t * 128 + 128), False))
        for g in range(3):
            vg.append(((3 * g + 2) * 1024 + ar(h0, h0 + 512), [(h, h % 2) for h in range(8)]))
    return fm, vg


def layer_dims(layer):
    fm, vg = layer_cols(layer, 0)
    nfm = len(fm)
    nv = [len(v[0]) for v in vg]
    nvirt = [len(v[1]) for v in vg]
    return nfm, nv, nvirt


class Prog:
    def __init__(self, S):
        self.S = S
        self.nc = bass.Bass("TRN2", target_bir_lowering=False)
        self.sc = Sched(self.nc)
        nc = self.nc
        self.arena = nc.alloc_sbuf_tensor("arena", [128, ARENA_BYTES // 2], BF16)
        self.consb = nc.alloc_sbuf_tensor("consb", [128, NCONS], BF16)
        self.identf = nc.alloc_sbuf_tensor("identf", [128, 128], F32)
        self.ps = [nc.alloc_psum_tensor("ps%d" % i, [128, 512], F32) for i in range(8)]
        self.aoff = 0
        self.uid = 0
        self.phase = "init"

    def reset(self, phase):
        self.sc.barrier()
        self.aoff = 0
        self.phase = phase

    def alloc(self, n, dt, name):
        nb = n * esz(dt)
        off = self.aoff
        self.aoff += (nb + 63) // 64 * 64
        assert self.aoff <= ARENA_BYTES, (self.phase, name, self.aoff)
        v = self.arena[:, off // 2:(off + nb) // 2]
        if dt == F32:
            v = v.bitcast(F32)
        self.uid += 1
        return v, "%s:%s:%d" % (self.phase, name, self.uid)

    def allocn(self, k, n, dt, name):
        return [self.alloc(n, dt, name + str(i)) for i in range(k)]

    def dram(self, name, shape, dt, kind="Internal"):
        return self.nc.dram_tensor(name, list(shape), dt, kind=kind).ap()

    def dma(self, out, in_, r, w, dsem, q="sp"):
        self.sc.op(q, lambda e: e.dma_start(out=out, in_=in_), r=r, w=w, dsem=dsem)

    def mm(self, out, lhsT, rhs, start, stop, r, w):
        self.sc.op("pe", lambda e: e.matmul(out, lhsT, rhs, start=start, stop=stop), r=r, w=w)

    def tr(self, out, in_, ident, r, w):
        self.sc.op("pe", lambda e: e.transpose(out, in_, ident), r=r, w=w)

    def act(self, out, in_, func, r, w, scale=1.0, bias=0.0):
        self.sc.op("act", lambda e: e.activation(out, in_, func, bias=bias, scale=scale), r=r, w=w)

    def copy(self, eng, out, in_, r, w):
        if eng == "act":
            self.sc.op("act", lambda e: e.copy(out, in_), r=r, w=w)
        else:
            self.sc.op(eng, lambda e: e.tensor_copy(out, in_), r=r, w=w)

    def tt(self, eng, out, a, b, op, r, w):
        self.sc.op(eng, lambda e: e.tensor_tensor(out, a, b, op), r=r, w=w)

    def memset(self, eng, ap, val, w):
        self.sc.op(eng, lambda e: e.memset(ap, val), r=(), w=w)

    def load_consts(self, cons_d):
        self.reset("cons")
        stg, k = self.alloc(NCONS, F32, "cstg")
        self.dma(stg, cons_d, r=[], w=[k], dsem="cstg")
        self.copy("dve", self.consb[:, :], stg, r=[k], w=["consb"])
        self.copy("act", self.identf[:, :], stg[:, C_I:C_I + 128], r=[k], w=["identf"])

    def cb(self, off, n=128, rows=128):
        return self.consb[0:rows, off:off + n]

    def p1(self, layer, x_d, xkey, w_d, cos_d, sin_d, fm_d, vva_ds, lname):
        S = self.S
        fm, vg = layer_cols(layer, 0)
        nfm = len(fm)
        Fc = nfm * 128 + sum(len(v[0]) for v in vg)
        self.reset("p1_%d" % layer)
        W, kW = self.alloc(8 * Fc, BF16, "W")
        W3 = W.rearrange("p (c f) -> p c f", c=8)
        wst = self.allocn(2, 2048, F32, "wst")
        xin = self.allocn(2, 4 * 1024, F32, "xin")
        xT = self.allocn(2, 8 * 512, BF16, "xT")
        rope_any = any(f[2] for f in fm)
        if rope_any:
            cst = self.allocn(2, 512, F32, "cos")
            sst = self.allocn(2, 512, F32, "sin")
            qb = self.allocn(2, 512, BF16, "qb")
            t1 = self.allocn(2, 512, F32, "t1")
            t2 = self.allocn(2, 512, F32, "t2")
        stg = self.allocn(3, 512, BF16, "stg")
        vst = self.allocn(3, 1024, BF16, "vst")
        if layer == 1:
            kms, kkms = self.alloc(4 * 32, F32, "kms")
            kms3 = kms.rearrange("p (a n) -> p a n", a=4)
        for (v, k) in vst:
            self.memset("pool", v, 1.0, w=[k])
        wv = w_d.rearrange("(c p) f -> p c f", p=128)
        pi = 0
        for c in range(8):
            for f0 in range(0, Fc, 2048):
                fw = min(2048, Fc - f0)
                s_, ks = wst[pi % 2]
                self.dma(s_[:, 0:fw], wv[:, c, f0:f0 + fw], r=[], w=[ks], dsem=ks)
                self.copy("dve" if pi % 2 == 0 else "pool", W3[:, c, f0:f0 + fw], s_[:, 0:fw], r=[ks], w=[kW])
                pi += 1
        ntt = S // 512
        si = 0
        vi = 0
        ri = 0
        psi = 0
        for tt in range(ntt):
            xi, kxi = xin[tt % 2]
            xi3 = xi.rearrange("p (s c) -> p s c", s=4)
            self.dma(xi3, x_d[tt * 512:(tt + 1) * 512, :].rearrange("(s p) c -> p s c", p=128),
                     r=[xkey], w=[kxi], dsem=kxi)
            xt, kxt = xT[tt % 2]
            xt3 = xt.rearrange("p (c t) -> p c t", c=8)
            for c in range(8):
                pb = self.ps[psi % 2]
                kp = ("ps", psi % 2)
                psi += 1
                for s in range(4):
                    self.tr(pb[:, s * 128:(s + 1) * 128], xi3[:, s, c * 128:(c + 1) * 128], self.identf[:, :],
                            r=[kxi, "identf"], w=[kp])
                self.copy("act" if c % 2 == 0 else "dve", xt3[:, c, :], pb[:, :], r=[kp], w=[kxt])
            if rope_any:
                cs_, kcs = cst[tt % 2]
                sn_, ksn = sst[tt % 2]
                self.dma(cs_, cos_d[:, tt * 512:(tt + 1) * 512], r=[], w=[kcs], dsem=kcs)
                self.dma(sn_, sin_d[:, tt * 512:(tt + 1) * 512], r=[], w=[ksn], dsem=ksn)
            for fi, (kind, _, rope) in enumerate(fm):
                b = 2 + (fi % 3)
                pb = self.ps[b]
                kp = ("ps", b)
                for c in range(8):
                    self.mm(pb[:, :], W3[:, c, fi * 128:(fi + 1) * 128], xt3[:, c, :], start=(c == 0), stop=(c == 7),
                            r=[kW, kxt], w=[kp])
                so, kso = stg[si % 3]
                si += 1
                sc_ = 0.125 if kind == "q" else 1.0
                if kind == "z":
                    self.act(so, pb[:, :], AF.Silu, r=[kp], w=[kso])
                elif not rope:
                    self.act(so, pb[:, :], AF.Copy, r=[kp], w=[kso], scale=sc_)
                else:
                    qb_, kqb = qb[ri % 2]
                    t1_, kt1 = t1[ri % 2]
                    t2_, kt2 = t2[ri % 2]
                    ri += 1
                    self.act(qb_, pb[:, :], AF.Copy, r=[kp], w=[kqb], scale=sc_)
                    pr = self.ps[5 + (ri % 2)]
                    kpr = ("ps", 5 + (ri % 2))
                    self.mm(pr[:, :], self.cb(C_PM), qb_, start=True, stop=True, r=["consb", kqb], w=[kpr])
                    self.tt("dve", t1_, pr[:, :], sn_, ALU.mult, r=[kpr, ksn], w=[kt1])
                    self.sc.op("dve", (lambda e, o=t2_, i0=pb[:, :], s=sc_, i1=cs_:
                                       e.scalar_tensor_tensor(o, i0, s, i1, ALU.mult, ALU.mult)),
                               r=[kp, kcs], w=[kt2])
                    self.tt("pool", so, t1_, t2_, ALU.add, r=[kt1, kt2], w=[kso])
                    if layer == 1 and kind == "k":
                        kt_i = fi - 4
                        self.sc.op("dve", (lambda e, o=kms3[:, kt_i, 2 * tt:2 * tt + 2],
                                           i=so.rearrange("p (a b) -> p a b", a=2):
                                           e.tensor_reduce(o, i, AX.X, ALU.add)),
                                   r=[kso], w=[kkms])
                self.dma(fm_d[fi * 128:(fi + 1) * 128, tt * 512:(tt + 1) * 512], so, r=[kso],
                         w=["D:fm" + lname], dsem=kso)
            voff = nfm * 128
            for gi, (vc, virt) in enumerate(vg):
                nvc = len(vc)
                for s in range(4):
                    b = 2 + ((gi * 4 + s) % 3)
                    pb = self.ps[b]
                    kp = ("ps", b)
                    for c in range(8):
                        self.mm(pb[:, 0:nvc], xt3[:, c, s * 128:(s + 1) * 128], W3[:, c, voff:voff + nvc],
                                start=(c == 0), stop=(c == 7), r=[kW, kxt], w=[kp])
                    vs, kvs = vst[vi % 3]
                    vi += 1
                    nvirt = len(virt)
                    vs3 = vs[:, 0:nvirt * 128].rearrange("p (h e) -> p h e", e=128)
                    for form in (0, 1):
                        idxs = [i for i, (sh, fo) in enumerate(virt) if fo == form]
                        if not idxs:
                            continue
                        i0 = idxs[0]
                        st_i = (idxs[1] - idxs[0]) if len(idxs) > 1 else 1
                        s0 = virt[i0][0]
                        st_s = (virt[idxs[1]][0] - s0) if len(idxs) > 1 else 1
                        n_i = len(idxs)
                        out_ap = vs3[:, i0:i0 + st_i * (n_i - 1) + 1:st_i, form * 64:form * 64 + 64]
                        src = pb[:, 0:nvc].rearrange("p (h e) -> p h e", e=64)
                        in_ap = src[:, s0:s0 + st_s * (n_i - 1) + 1:st_s, :]
                        self.copy("act" if form == 0 else "dve", out_ap, in_ap, r=[kp], w=[kvs])
                    t0 = tt * 512 + s * 128
                    self.dma(vva_ds[gi][t0:t0 + 128, :], vs[:, 0:nvirt * 128], r=[kvs],
                             w=["D:vva" + lname], dsem=kvs)
                voff += nvc
        if layer == 1:
            return kms3, kkms
        return None

    def finalize(self, src, ksrc, base, cols, SZ, kSZ, YG, kYG, q0, rec2, yt2, fi, es=None):
        nb, db = base, 64 - base
        rc, krc = rec2[fi % 2]
        yt, kyt = yt2[fi % 2]
        if es is not None:
            es_ap, kes = es
            self.sc.op("dve", (lambda e, o=rc[nb:nb + 64, 0:cols], i=src[db:db + 64, 0:cols], s=es_ap[nb:nb + 64, :]:
                               e.tensor_scalar(o, i, s, None, ALU.add)), r=[ksrc, kes], w=[krc])
            self.sc.op("dve", lambda e, o=rc[nb:nb + 64, 0:cols]: e.reciprocal(o, o), r=[krc], w=[krc])
        else:
            self.sc.op("dve", lambda e, o=rc[nb:nb + 64, 0:cols], i=src[db:db + 64, 0:cols]: e.reciprocal(o, i),
                       r=[ksrc], w=[krc])
        self.tt("dve", yt[nb:nb + 64, 0:cols], src[nb:nb + 64, 0:cols], rc[nb:nb + 64, 0:cols], ALU.mult,
                r=[ksrc, krc], w=[kyt])
        self.tt("pool", YG[nb:nb + 64, q0:q0 + cols], yt[nb:nb + 64, 0:cols], SZ[nb:nb + 64, q0:q0 + cols], ALU.mult,
                r=[kyt, kSZ], w=[kYG])

    def p2_sb(self, fm_d, vva_d, yg_d, lname):
        S = self.S
        nkb = S // 128
        self.reset("p2sb")
        QT, kQT = self.alloc(S, BF16, "QT")
        KT, kKT = self.alloc(S, BF16, "KT")
        SZ, kSZ = self.alloc(S, BF16, "SZ")
        YG, kYG = self.alloc(S, BF16, "YG")
        VA, kVA = self.alloc(nkb * 256, BF16, "VA")
        VA4 = VA.rearrange("p (k f e) -> p k f e", f=2, e=128)
        ones, kon = self.alloc(512, F32, "ones")
        U = self.allocn(2, 512, F32, "U")
        SX = self.allocn(2, 516, F32, "SX")
        A = self.allocn(2, 512, BF16, "A")
        AT = self.allocn(2, 512, BF16, "AT")
        self.memset("pool", ones, 1.0, w=[kon])
        step = 0
        oti = 0
        for hp in range(4):
            self.dma(QT, fm_d[hp * 128:(hp + 1) * 128, :], r=["D:fm" + lname], w=[kQT], dsem=kQT)
            self.dma(KT, fm_d[(4 + hp) * 128:(5 + hp) * 128, :], r=["D:fm" + lname], w=[kKT], dsem=kKT)
            self.dma(SZ, fm_d[(8 + hp) * 128:(9 + hp) * 128, :], r=["D:fm" + lname], w=[kSZ], dsem=kSZ)
            self.dma(VA4, vva_d[:, hp * 256:(hp + 1) * 256].rearrange("(k p) (f e) -> p k f e", p=128, e=128),
                     r=["D:vva" + lname], w=[kVA], dsem=kVA)
            for hl in range(2):
                base = 64 * hl
                for qb in range(nkb):
                    c0 = qb * 128
                    nch = (S - c0 + 511) // 512
                    ot = self.ps[4 + oti % 2]
                    kot = ("ps", 4 + oti % 2)
                    oti += 1
                    prev = None
                    for j in range(nch):
                        k0 = c0 + 512 * j
                        wd = min(512, S - k0)
                        zb = self.ps[step % 2]
                        kz = ("ps", step % 2)
                        tb = self.ps[2 + step % 2]
                        ktb = ("ps", 2 + step % 2)
                        tbb = tb[:, 0:256].bitcast(BF16)
                        u_, ku = U[step % 2]
                        sx, ksx = SX[step % 2]
                        a_, ka = A[step % 2]
                        at, kat = AT[step % 2]
                        self.mm(zb[:, 0:wd], QT[base:base + 64, c0:c0 + 128], KT[base:base + 64, k0:k0 + wd],
                                start=True, stop=(j != 0), r=[kQT, kKT], w=[kz])
                        if j == 0:
                            self.mm(zb[:, 0:128], self.cb(C_I), self.cb(C_SBM), start=False, stop=True,
                                    r=["consb"], w=[kz])
                        self.act(u_[:, 0:wd], zb[:, 0:wd], AF.Sigmoid, r=[kz], w=[ku], scale=-1.0)
                        if j == 0:
                            self.memset("pool", sx[:, 0:1], 1.0, w=[ksx])
                        else:
                            psx, kpsx, pwd = prev
                            self.copy("pool", sx[:, 0:1], psx[:, pwd:pwd + 1], r=[kpsx], w=[ksx])
                        self.sc.op("dve", (lambda e, o=sx[:, 1:wd + 1], d0=u_[:, 0:wd], d1=ones[:, 0:wd], ini=sx[:, 0:1]:
                                           e.tensor_tensor_scan(o, d0, d1, ini, ALU.mult, ALU.mult)),
                                   r=[ku, kon, ksx], w=[ksx])
                        self.tt("pool", a_[:, 0:wd], sx[:, 0:wd], sx[:, 1:wd + 1], ALU.subtract, r=[ksx], w=[ka])
                        nm = wd // 128
                        for m in range(nm):
                            self.tr(tbb[:, m * 128:(m + 1) * 128], a_[:, m * 128:(m + 1) * 128], self.cb(C_I),
                                    r=[ka, "consb"], w=[ktb])
                        self.copy("act" if step % 2 == 0 else "dve", at[:, 0:wd], tbb[:, 0:wd], r=[ktb], w=[kat])
                        for m in range(nm):
                            kb = (k0 // 128) + m
                            self.mm(ot[:, 0:128], VA4[:, kb, hl, :], at[:, m * 128:(m + 1) * 128],
                                    start=(j == 0 and m == 0), stop=(j == nch - 1 and m == nm - 1),
                                    r=[kVA, kat], w=[kot])
                        prev = (sx, ksx, wd)
                        step += 1
                    self.tt("dve", YG[base:base + 64, c0:c0 + 128], ot[base:base + 64, 0:128],
                            SZ[base:base + 64, c0:c0 + 128], ALU.mult, r=[kot, kSZ], w=[kYG])
            self.dma(yg_d[hp * 128:(hp + 1) * 128, :], YG, r=[kYG], w=["D:yg" + lname], dsem=kYG)

    def p2_moba(self, fm_d, vva_d, yg_d, kms3, kkms, lname):
        S = self.S
        nkb = S // 128
        nblk = S // 256
        self.sc.barrier()
        kmb_t = self.nc.alloc_sbuf_tensor("kmb", [128, 4 * 32], BF16)
        kmb = kmb_t[:, :].rearrange("p (a n) -> p a n", a=4)
        self.copy("dve", kmb, kms3, r=[kkms], w=["kmb"])
        self.reset("p2moba")
        QT, kQT = self.alloc(S, BF16, "QT")
        KT, kKT = self.alloc(S, BF16, "KT")
        SZ, kSZ = self.alloc(S, BF16, "SZ")
        YG, kYG = self.alloc(S, BF16, "YG")
        VA, kVA = self.alloc(nkb * 256, BF16, "VA")
        VA4 = VA.rearrange("p (k f e) -> p k f e", f=2, e=128)
        P = self.allocn(3, 512, BF16, "P")
        Gs, kGs = self.alloc(4 * 32, F32, "Gs")
        Gs3 = Gs.rearrange("p (a n) -> p a n", a=4)
        M8, kM8 = self.alloc(4 * 8, F32, "M8")
        M83 = M8.rearrange("p (a n) -> p a n", a=4)
        NMt, kNMt = self.alloc(4 * 32, F32, "NMt")
        NMt3 = NMt.rearrange("p (a n) -> p a n", a=4)
        NM = self.allocn(2, 512, BF16, "NM")
        rec2 = self.allocn(2, 512, F32, "rec")
        yt2 = self.allocn(2, 512, F32, "yt")
        step = 0
        fi = 0
        for hp in range(4):
            self.dma(QT, fm_d[hp * 128:(hp + 1) * 128, :], r=["D:fm" + lname], w=[kQT], dsem=kQT)
            self.dma(KT, fm_d[(4 + hp) * 128:(5 + hp) * 128, :], r=["D:fm" + lname], w=[kKT], dsem=kKT)
            self.dma(SZ, fm_d[(8 + hp) * 128:(9 + hp) * 128, :], r=["D:fm" + lname], w=[kSZ], dsem=kSZ)
            self.dma(VA4, vva_d[:, hp * 256:(hp + 1) * 256].rearrange("(k p) (f e) -> p k f e", p=128, e=128),
                     r=["D:vva" + lname], w=[kVA], dsem=kVA)
            for hl in range(2):
                base = 64 * hl
                for qt in range(S // 512):
                    q0 = qt * 512
                    gp = self.ps[5]
                    kgp = ("ps", 5)
                    gp3 = gp[:, 0:128].rearrange("p (a n) -> p a n", a=4)
                    for sub in range(4):
                        tb = 4 * qt + sub
                        self.mm(gp3[:, sub, 0:nblk], QT[base:base + 64, tb * 128:(tb + 1) * 128],
                                kmb[base:base + 64, hp, 0:nblk], start=True, stop=True, r=[kQT, "kmb"], w=[kgp])
                    self.memset("pool", Gs, -1.0e30, w=[kGs])
                    for sub in range(4):
                        qblk = (4 * qt + sub) // 2
                        if qblk > 0:
                            self.copy("dve", Gs3[:, sub, 0:qblk], gp3[:, sub, 0:qblk], r=[kgp], w=[kGs])
                    for sub in range(4):
                        self.sc.op("dve", lambda e, o=M83[:, sub, :], i=Gs3[:, sub, :]: e.max(o, i), r=[kGs], w=[kM8])
                    for sub in range(4):
                        self.sc.op("dve", (lambda e, o=NMt3[:, sub, :], i=Gs3[:, sub, :], s=M83[:, sub, 2:3]:
                                           e.tensor_scalar(o, i, s, -BIG, ALU.is_lt, ALU.mult)),
                                   r=[kGs, kM8], w=[kNMt])
                    for sub in range(4):
                        qblk = (4 * qt + sub) // 2
                        self.memset("pool", NMt3[:, sub, qblk:qblk + 1], 0.0, w=[kNMt])
                        if qblk + 1 < 32:
                            self.memset("pool", NMt3[:, sub, qblk + 1:32], -BIG, w=[kNMt])
                    np_ = self.ps[6]
                    knp = ("ps", 6)
                    for sub in range(4):
                        self.tr(np_[0:32, sub * 128:(sub + 1) * 128], NMt3[:, sub, :], self.identf[:, :],
                                r=[kNMt, "identf"], w=[knp])
                    nm_, knm = NM[fi % 2]
                    self.copy("act", nm_[0:32, :], np_[0:32, :], r=[knp], w=[knm])
                    ot = self.ps[3 + fi % 2]
                    kot = ("ps", 3 + fi % 2)
                    nkeys = 4 * qt + 4
                    for kb in range(nkeys):
                        j = kb - 4 * qt
                        c0 = 128 * j if j > 0 else 0
                        n = kb // 2
                        sb_ = self.ps[step % 3]
                        ksb = ("ps", step % 3)
                        p_, kp_ = P[step % 3]
                        step += 1
                        self.mm(sb_[:, c0:512], KT[base:base + 64, kb * 128:(kb + 1) * 128],
                                QT[base:base + 64, q0 + c0:q0 + 512], start=True, stop=False, r=[kKT, kQT], w=[ksb])
                        self.mm(sb_[:, c0:512], self.cb(C_ESEL + n * 128, 128, 32), nm_[0:32, c0:512],
                                start=False, stop=(j < 0), r=["consb", knm], w=[ksb])
                        if j >= 0:
                            self.mm(sb_[:, c0:c0 + 128], self.cb(C_I), self.cb(C_CAUS), start=False, stop=True,
                                    r=["consb"], w=[ksb])
                        self.act(p_[:, c0:512], sb_[:, c0:512], AF.Exp, r=[ksb], w=[kp_])
                        self.mm(ot[:, c0:512], VA4[:, kb, hl, :], p_[:, c0:512], start=(kb == 0),
                                stop=(kb == nkeys - 1), r=[kVA, kp_], w=[kot])
                    self.finalize(ot, kot, base, 512, SZ, kSZ, YG, kYG, q0, rec2, yt2, fi)
                    fi += 1
            self.dma(yg_d[hp * 128:(hp + 1) * 128, :], YG, r=[kYG], w=["D:yg" + lname], dsem=kYG)

    def p2_swa(self, fm_d, vva_d, yg_d, es_d, hh, lname):
        S = self.S
        nkb = S // 128
        self.reset("p2swa")
        QT, kQT = self.alloc(S, BF16, "QT")
        KT, kKT = self.alloc(S, BF16, "KT")
        SZ, kSZ = self.alloc(S, BF16, "SZ")
        YG, kYG = self.alloc(S, BF16, "YG")
        VA, kVA = self.alloc(nkb * 256, BF16, "VA")
        VA4 = VA.rearrange("p (k f e) -> p k f e", f=2, e=128)
        P = self.allocn(4, 512, BF16, "P")
        ES, kES = self.alloc(16, F32, "ES")
        rec2 = self.allocn(2, 512, F32, "rec")
        yt2 = self.allocn(2, 512, F32, "yt")
        self.dma(ES, es_d, r=[], w=[kES], dsem=kES)
        self.act(ES, ES, AF.Exp, r=[kES], w=[kES])
        step = 0
        fi = 0
        for hp in range(4):
            g = hp // 2
            self.dma(QT, fm_d[hp * 128:(hp + 1) * 128, :], r=["D:fm" + lname], w=[kQT], dsem=kQT)
            if hp % 2 == 0:
                self.dma(KT, fm_d[(4 + g) * 128:(5 + g) * 128, :], r=["D:fm" + lname], w=[kKT], dsem=kKT)
                self.dma(VA4, vva_d[:, g * 256:(g + 1) * 256].rearrange("(k p) (f e) -> p k f e", p=128, e=128),
                         r=["D:vva" + lname], w=[kVA], dsem=kVA)
            self.dma(SZ, fm_d[(6 + hp) * 128:(7 + hp) * 128, :], r=["D:fm" + lname], w=[kSZ], dsem=kSZ)
            for hl in range(2):
                base = 64 * hl
                hglob = hh * 8 + hp * 2 + hl
                for qt in range(S // 512):
                    q0 = qt * 512
                    ot = self.ps[4 + fi % 2]
                    kot = ("ps", 4 + fi % 2)
                    banks = []
                    for which in (0, 1):
                        sb_ = self.ps[step % 4]
                        ksb = ("ps", step % 4)
                        p_, kp_ = P[step % 4]
                        step += 1
                        jlo = 4
                        for jb in range(4):
                            qb = 4 * qt + jb
                            kb = qb - 1 + which
                            if kb < 0:
                                continue
                            jlo = min(jlo, jb)
                            self.mm(sb_[:, jb * 128:(jb + 1) * 128], KT[base:base + 64, kb * 128:(kb + 1) * 128],
                                    QT[base:base + 64, qb * 128:(qb + 1) * 128], start=True, stop=False,
                                    r=[kKT, kQT], w=[ksb])
                            self.mm(sb_[:, jb * 128:(jb + 1) * 128], self.cb(C_I),
                                    self.cb(C_SWAP if which == 0 else C_CAUS), start=False, stop=True,
                                    r=["consb"], w=[ksb])
                        self.act(p_[:, jlo * 128:512], sb_[:, jlo * 128:512], AF.Exp, r=[ksb], w=[kp_])
                        banks.append((p_, kp_))
                    for jb in range(4):
                        qb = 4 * qt + jb
                        first = True
                        for which in (0, 1):
                            kb = qb - 1 + which
                            if kb < 0:
                                continue
                            p_, kp_ = banks[which]
                            self.mm(ot[:, jb * 128:(jb + 1) * 128], VA4[:, kb, hl, :], p_[:, jb * 128:(jb + 1) * 128],
                                    start=first, stop=(which == 1), r=[kVA, kp_], w=[kot])
                            first = False
                    self.finalize(ot, kot, base, 512, SZ, kSZ, YG, kYG, q0, rec2, yt2, fi,
                                  es=(ES[:, hglob:hglob + 1], kES))
                    fi += 1
            self.dma(yg_d[hp * 128:(hp + 1) * 128, :], YG, r=[kYG], w=["D:yg" + lname], dsem=kYG)

    def p2_dil(self, fm_d, vva_ds, yg_d, lname):
        S = self.S
        nkb = S // 128
        self.reset("p2dil")
        QT, kQT = self.alloc(S, BF16, "QT")
        KT, kKT = self.alloc(S, BF16, "KT")
        SZ, kSZ = self.alloc(S, BF16, "SZ")
        YG, kYG = self.alloc(S, BF16, "YG")
        VA, kVA = self.alloc(nkb * 256, BF16, "VA")
        VA4 = VA.rearrange("p (k f e) -> p k f e", f=2, e=128)
        ACC = self.allocn(2, S, F32, "ACC")
        P = self.allocn(4, 512, BF16, "P")
        rec2 = self.allocn(2, 512, F32, "rec")
        yt2 = self.allocn(2, 512, F32, "yt")
        step = 0
        fi = 0
        oti = 0
        nspan = S // 2048
        for hp in range(4):
            self.dma(SZ, fm_d[(24 + hp) * 128:(25 + hp) * 128, :], r=["D:fm" + lname], w=[kSZ], dsem=kSZ)
            for g, (win, d) in enumerate(DIL):
                L = S // d
                nI = L // 128
                self.dma(QT, fm_d[(8 * g + hp) * 128:(8 * g + hp + 1) * 128, :], r=["D:fm" + lname], w=[kQT], dsem=kQT)
                self.dma(KT, fm_d[(8 * g + 4 + hp) * 128:(8 * g + 5 + hp) * 128, :], r=["D:fm" + lname], w=[kKT],
                         dsem=kKT)
                for r_ in range(d):
                    src = vva_ds[g][r_:S:d, hp * 256:(hp + 1) * 256] if d > 1 else vva_ds[g][:, hp * 256:(hp + 1) * 256]
                    self.dma(VA4[:, r_ * nI:(r_ + 1) * nI, :, :],
                             src.rearrange("(k p) (f e) -> p k f e", p=128, e=128),
                             r=["D:vva" + lname], w=[kVA], dsem=kVA)
                bps = 2048 // d // 128
                for hl in range(2):
                    base = 64 * hl
                    acc, kacc = ACC[hl]
                    for sp_ in range(nspan):
                        s0 = sp_ * 2048
                        for bk in range(4):
                            ot = self.ps[4 + oti % 2]
                            kot = ("ps", 4 + oti % 2)
                            oti += 1
                            blocks = []
                            for jb in range(4):
                                qi = 4 * bk + jb
                                r_ = qi // bps
                                I = sp_ * bps + (qi % bps)
                                blocks.append((r_, I))
                            banks = []
                            for which in (0, 1):
                                sb_ = self.ps[step % 4]
                                ksb = ("ps", step % 4)
                                p_, kp_ = P[step % 4]
                                step += 1
                                jl = [jb for jb in range(4) if blocks[jb][1] - 1 + which >= 0]
                                for jb in jl:
                                    r_, I = blocks[jb]
                                    Ik = I - 1 + which
                                    tq = r_ + d * 128 * I
                                    tk = r_ + d * 128 * Ik
                                    self.mm(sb_[:, jb * 128:(jb + 1) * 128],
                                            KT[base:base + 64, tk:tk + d * 127 + 1:d],
                                            QT[base:base + 64, tq:tq + d * 127 + 1:d], start=True, stop=False,
                                            r=[kKT, kQT], w=[ksb])
                                    self.mm(sb_[:, jb * 128:(jb + 1) * 128], self.cb(C_I),
                                            self.cb(C_DILP if which == 0 else C_CAUS), start=False, stop=True,
                                            r=["consb"], w=[ksb])
                                if jl:
                                    runs = []
                                    for jb in jl:
                                        if runs and runs[-1][1] == jb:
                                            runs[-1][1] = jb + 1
                                        else:
                                            runs.append([jb, jb + 1])
                                    for a, b in runs:
                                        self.act(p_[:, a * 128:b * 128], sb_[:, a * 128:b * 128], AF.Exp, r=[ksb],
                                                 w=[kp_])
                                banks.append((p_, kp_, jl))
                            for jb in range(4):
                                r_, I = blocks[jb]
                                first = True
                                for which in (0, 1):
                                    Ik = I - 1 + which
                                    if Ik < 0:
                                        continue
                                    p_, kp_, _ = banks[which]
                                    self.mm(ot[:, jb * 128:(jb + 1) * 128], VA4[:, r_ * nI + Ik, hl, :],
                                            p_[:, jb * 128:(jb + 1) * 128], start=first, stop=(which == 1),
                                            r=[kVA, kp_], w=[kot])
                                    first = False
                            span = acc[:, s0:s0 + 2048]
                            if d == 1:
                                dst = span[:, bk * 512:(bk + 1) * 512]
                                srcp = ot[:, :]
                            elif d == 4:
                                dst = span.rearrange("p (l r) -> p r l", r=4)[:, bk, :]
                                srcp = ot[:, :]
                            else:
                                dst = span.rearrange("p (l r) -> p r l", r=16)[:, 4 * bk:4 * bk + 4, :]
                                srcp = ot[:, :].rearrange("p (j l) -> p j l", j=4)
                            if g == 0:
                                self.copy("dve", dst, srcp, r=[kot], w=[kacc])
                            else:
                                self.tt("dve", dst, dst, srcp, ALU.add, r=[kot, kacc], w=[kacc])
            for hl in range(2):
                base = 64 * hl
                acc, kacc = ACC[hl]
                for qt in range(S // 512):
                    self.finalize(acc[:, qt * 512:(qt + 1) * 512], kacc, base, 512, SZ, kSZ, YG, kYG, qt * 512,
                                  rec2, yt2, fi)
                    fi += 1
            self.dma(yg_d[hp * 128:(hp + 1) * 128, :], YG, r=[kYG], w=["D:yg" + lname], dsem=kYG)

    def p3(self, T, x_d, xkey, yg_parts, ygkeys, wout_d, g_d, b_d, out_d, outkey, rev_out, lname):
        self.reset("p3" + lname)
        WO, kWO = self.alloc(8 * 1024, BF16, "WO")
        WO3 = WO.rearrange("p (c f) -> p c f", c=8)
        wst = self.allocn(2, 2048, F32, "wst")
        G, kG = self.alloc(1024, F32, "G")
        Bt, kBt = self.alloc(1024, F32, "Bt")
        yg = self.allocn(2, 8 * 512, BF16, "yg")
        xin = self.allocn(2, 4 * 1024, F32, "xin")
        R = self.allocn(2, 1024, F32, "R")
        XN = self.allocn(2, 1024, F32, "XN")
        O = self.allocn(2, 1024, F32, "O")
        ST = self.allocn(2, 16, F32, "ST")
        MV = self.allocn(2, 4, F32, "MV")
        wv = wout_d.rearrange("(c p) f -> p c f", p=128)
        pi = 0
        for c in range(8):
            s_, ks = wst[pi % 2]
            self.dma(s_[:, 0:1024], wv[:, c, :], r=[], w=[ks], dsem=ks)
            self.copy("dve" if pi % 2 == 0 else "pool", WO3[:, c, :], s_[:, 0:1024], r=[ks], w=[kWO])
            pi += 1
        self.dma(G, g_d, r=[], w=[kG], dsem=kG)
        self.dma(Bt, b_d, r=[], w=[kBt], dsem=kBt)
        it = 0
        for tt in range(T // 512):
            y_, ky = yg[tt % 2]
            y3 = y_.rearrange("p (c t) -> p c t", c=8)
            for half in range(2):
                self.dma(y3[:, 4 * half:4 * half + 4, :],
                         yg_parts[half][:, tt * 512:(tt + 1) * 512].rearrange("(c p) t -> p c t", p=128),
                         r=[ygkeys[half]], w=[ky + "h%d" % half], dsem=ky + "h%d" % half)
            xi, kxi = xin[tt % 2]
            xi3 = xi.rearrange("p (s c) -> p s c", s=4)
            self.dma(xi3, x_d[tt * 512:(tt + 1) * 512, :].rearrange("(s p) c -> p s c", p=128),
                     r=[xkey], w=[kxi], dsem=kxi)
            for s in range(4):
                r_, kr = R[it % 2]
                xn, kxn = XN[it % 2]
                o_, ko = O[it % 2]
                st_, kst = ST[it % 2]
                mv, kmv = MV[it % 2]
                st3 = st_[:, 0:12].rearrange("p (a b) -> p a b", a=2)
                for hf in range(2):
                    b = (2 * it + hf) % 4
                    pb = self.ps[b]
                    kp = ("ps", b)
                    for c in range(8):
                        self.mm(pb[:, :], y3[:, c, s * 128:(s + 1) * 128], WO3[:, c, hf * 512:(hf + 1) * 512],
                                start=(c == 0), stop=(c == 7), r=[ky + "h0", ky + "h1", kWO], w=[kp])
                    self.sc.op("dve", (lambda e, o=r_[:, hf * 512:(hf + 1) * 512], i0=xi3[:, s, hf * 512:(hf + 1) * 512],
                                       i1=pb[:, :]: e.scalar_tensor_tensor(o, i0, float(ALPHA), i1, ALU.mult, ALU.add)),
                               r=[kxi, kp], w=[kr + "h%d" % hf])
                    self.sc.op("dve", lambda e, o=st3[:, hf, :], i=r_[:, hf * 512:(hf + 1) * 512]: e.bn_stats(o, i),
                               r=[kr + "h%d" % hf], w=[kst + "h%d" % hf])
                self.sc.op("dve", lambda e, o=mv[:, 0:2], i=st3: e.bn_aggr(o, i), r=[kst + "h0", kst + "h1"], w=[kmv])
                self.sc.op("dve", (lambda e, o=mv[:, 2:3], i=mv[:, 1:2]:
                                   e.tensor_scalar(o, i, float(EPS), None, ALU.add)), r=[kmv], w=[kmv])
                self.act(mv[:, 2:3], mv[:, 2:3], AF.Sqrt, r=[kmv], w=[kmv])
                self.sc.op("dve", lambda e, o=mv[:, 2:3]: e.reciprocal(o, o), r=[kmv], w=[kmv])
                self.sc.op("dve", (lambda e, o=xn, i=r_, s1=mv[:, 0:1], s2=mv[:, 2:3]:
                                   e.tensor_scalar(o, i, s1, s2, ALU.subtract, ALU.mult)),
                           r=[kr + "h0", kr + "h1", kmv], w=[kxn])
                self.tt("pool", xn, xn, G, ALU.mult, r=[kxn, kG], w=[kxn])
                self.tt("pool", o_, xn, Bt, ALU.add, r=[kxn, kBt], w=[ko])
                t0 = tt * 512 + s * 128
                if rev_out:
                    dst = out_d[T - t0 - 128:T - t0, :][::-1, :]
                else:
                    dst = out_d[t0:t0 + 128, :]
                self.dma(dst, o_, r=[ko], w=[outkey], dsem=ko)
                it += 1


def build_A(layer, S, hh_sym=0):
    pg = Prog(S)
    nfm, nv, nvirt = layer_dims(layer)
    Fc = nfm * 128 + sum(nv)
    x_d = pg.dram("x", [S, D], F32, "ExternalInput")
    w_d = pg.dram("w", [D, Fc], F32, "ExternalInput")
    cons_d = pg.dram("cons", [128, NCONS], F32, "ExternalInput")
    cos_d = pg.dram("cos", [128, S], F32, "ExternalInput")
    sin_d = pg.dram("sin", [128, S], F32, "ExternalInput")
    es_d = pg.dram("es", [128, 16], F32, "ExternalInput")
    yg_d = pg.dram("yg", [512, S], BF16, "ExternalOutput")
    fm_d = pg.dram("fm", [nfm * 128, S], BF16)
    vva_ds = [pg.dram("vva%d" % i, [S, nvirt[i] * 128], BF16) for i in range(len(nv))]
    ln = str(layer)
    pg.load_consts(cons_d)
    km = pg.p1(layer, x_d, "D:x", w_d, cos_d, sin_d, fm_d, vva_ds, ln)
    if layer == 0:
        pg.p2_sb(fm_d, vva_ds[0], yg_d, ln)
    elif layer == 1:
        pg.p2_moba(fm_d, vva_ds[0], yg_d, km[0], km[1], ln)
    elif layer == 2:
        pg.p2_swa(fm_d, vva_ds[0], yg_d, es_d, 0, ln)
    else:
        pg.p2_dil(fm_d, vva_ds, yg_d, ln)
    pg.sc.finish()
    pg.sc.emit()
    return pg


def build_B(T):
    pg = Prog(T)
    x_d = pg.dram("x", [T, D], F32, "ExternalInput")
    yga = pg.dram("yga", [512, T], BF16, "ExternalInput")
    ygb = pg.dram("ygb", [512, T], BF16, "ExternalInput")
    wo_d = pg.dram("wo", [D, D], F32, "ExternalInput")
    g_d = pg.dram("g", [128, D], F32, "ExternalInput")
    b_d = pg.dram("b", [128, D], F32, "ExternalInput")
    out_d = pg.dram("out", [T, D], F32, "ExternalOutput")
    pg.p3(T, x_d, "D:x", [yga, ygb], ["D:yga", "D:ygb"], wo_d, g_d, b_d, out_d, "D:out", False, "b")
    pg.sc.finish()
    pg.sc.emit()
    return pg


_CACHE = {}


def _prog(key, fn):
    if key not in _CACHE:
        _CACHE[key] = fn()
    return _CACHE[key]


W_IN = ("sb_w_in", "moba_w_in", "swa_w_in", "dil_w_in")
W_OUT = ("sb_w_out", "moba_w_out", "swa_w_out", "dil_w_out")


def run_layer_A(layer, xs, w_in, sinks, S, cores=None):
    nb = len(xs)
    pg = _prog(("A", layer, S), lambda: build_A(layer, S))
    cons = make_consts()
    cos, sin = rope_tabs(S)
    in_maps = []
    for b in range(nb):
        for hh in range(2):
            fm, vg = layer_cols(layer, hh)
            cols = np.concatenate([f[1] for f in fm] + [v[0] for v in vg])
            es = np.zeros((128, 16), np.float32)
            if sinks is not None:
                es[:, 0:8] = np.broadcast_to(np.asarray(sinks, np.float32)[8 * hh:8 * hh + 8][None, :], (128, 8))
            in_maps.append(dict(x=np.ascontiguousarray(xs[b]), w=np.ascontiguousarray(w_in[:, cols]),
                                cons=cons, cos=cos, sin=sin, es=es))
    res = run_bass_kernel_spmd(pg.nc, in_maps, core_ids=list(range(len(in_maps))))
    out = []
    for b in range(nb):
        out.append(np.concatenate([res.results[2 * b]["yg"], res.results[2 * b + 1]["yg"]], axis=0))
    return out


def run_layer_B(xs, ygs, w_out, g, bvec, S):
    nb = len(xs)
    T = S // 2
    pg = _prog(("B", T), lambda: build_B(T))
    gt = np.ascontiguousarray(np.broadcast_to(np.asarray(g, np.float32)[None, :], (128, D)))
    bt = np.ascontiguousarray(np.broadcast_to(np.asarray(bvec, np.float32)[None, :], (128, D)))
    in_maps = []
    for b in range(nb):
        for hf in range(2):
            sl = slice(hf * T, (hf + 1) * T)
            in_maps.append(dict(x=np.ascontiguousarray(xs[b][sl]),
                                yga=np.ascontiguousarray(ygs[b][0:512, sl]),
                                ygb=np.ascontiguousarray(ygs[b][512:1024, sl]),
                                wo=np.ascontiguousarray(w_out), g=gt, b=bt))
    res = run_bass_kernel_spmd(pg.nc, in_maps, core_ids=list(range(len(in_maps))))
    return [np.concatenate([res.results[2 * b]["out"], res.results[2 * b + 1]["out"]], axis=0) for b in range(nb)]


def kernel(**inputs):
    x = np.asarray(inputs["x"], np.float32)
    B, S, _ = x.shape
    xs = [x[b] for b in range(B)]
    for layer in range(DEPTH):
        w_in = np.asarray(inputs[W_IN[layer]], np.float32)
        w_out = np.asarray(inputs[W_OUT[layer]], np.float32)
        g = inputs["ln%d_g" % layer]
        bv = inputs["ln%d_b" % layer]
        sinks = inputs["swa_sinks"] if layer == 2 else None
        if layer == 0:
            xin = [np.ascontiguousarray(a[::-1]) for a in xs]
        else:
            xin = xs
        ygs = run_layer_A(layer, xin, w_in, sinks, S)
        xo = run_layer_B(xin, ygs, w_out, g, bv, S)
        if layer == 0:
            xo = [np.ascontiguousarray(a[::-1]) for a in xo]
        xs = xo
    return np.stack(xs, axis=0).astype(np.float32)
```

```python
import math
from contextlib import ExitStack

import numpy as np
import concourse.bass as bass
import concourse.mybir as mybir
from concourse.bass_utils import run_bass_kernel_spmd

F32 = mybir.dt.float32
BF16 = mybir.dt.bfloat16
AF = mybir.ActivationFunctionType
ALU = mybir.AluOpType
AX = mybir.AxisListType

D = 1024
NH = 16
DH = 64
DEPTH = 4
NCORES = 8
ALPHA = (2.0 * DEPTH) ** 0.25
EPS = 1e-5
BIG = 30000.0
ROPE_THETA = 500000.0
DIL = ((128, 1), (512, 4), (2048, 16))

C_I, C_PM, C_SBM, C_CAUS, C_SWAP, C_DILP, C_ESEL = 0, 128, 256, 384, 512, 640, 768
C_NI = 768 + 32 * 128
NCONS = C_NI + 128

ARENA_BYTES = 176 * 1024


def esz(dt):
    return 4 if dt == F32 else 2


class Sched:
    ENGS = ("pe", "act", "dve", "pool", "sp")

    def __init__(self, nc):
        self.nc = nc
        self.ops = []
        self.lastw = {}
        self.readers = {}
        self.dw = {}
        self.dma_last = {}
        self.eng_last = {}
        self.bar = {e: set() for e in self.ENGS}
        self.slot_of = {}
        self.nslot = 0
        self.maxslot = 0

    def op(self, eng, fn, r=(), w=(), dsem=None, dinc=16):
        idx = len(self.ops)
        deps = set()
        for k in r:
            if isinstance(k, str) and k.startswith("D:"):
                deps.update(self.dw.get(k, {}).values())
            else:
                lw = self.lastw.get(k)
                if lw is not None:
                    deps.add(lw)
        for k in w:
            if isinstance(k, str) and k.startswith("D:"):
                continue
            lw = self.lastw.get(k)
            if lw is not None:
                deps.add(lw)
            deps.update(self.readers.get(k, ()))
        if self.bar[eng]:
            deps.update(self.bar[eng])
            self.bar[eng] = set()
        for k in r:
            if not (isinstance(k, str) and k.startswith("D:")):
                self.readers.setdefault(k, []).append(idx)
        for k in w:
            if isinstance(k, str) and k.startswith("D:"):
                self.dw.setdefault(k, {})[dsem] = idx
            else:
                self.lastw[k] = idx
                self.readers[k] = []
        slot = None
        if dsem is not None:
            if dsem not in self.slot_of:
                self.slot_of[dsem] = self.nslot
                self.nslot += 1
                self.maxslot = max(self.maxslot, self.nslot)
            slot = self.slot_of[dsem]
        self.ops.append(dict(eng=eng, fn=fn, deps=deps, dsem=dsem, dinc=dinc, slot=slot))
        if dsem is not None:
            self.dma_last[dsem] = idx
        else:
            self.eng_last[eng] = idx
        return idx

    def barrier(self):
        pts = set(self.eng_last.values()) | set(self.dma_last.values())
        for e in self.ENGS:
            self.bar[e] = set(pts)
        self.slot_of = {}
        self.nslot = 0

    def finish(self, final_eng="sp"):
        self.barrier()
        self.op(final_eng, lambda e: e.nop(), r=(), w=())

    def emit(self):
        nc = self.nc
        ops = self.ops
        n = len(ops)
        signal = [False] * n
        for i, o in enumerate(ops):
            for j in o["deps"]:
                oj = ops[j]
                if oj["dsem"] is not None:
                    continue
                if oj["eng"] == o["eng"] and o["eng"] in ("pe", "sp") and o["dsem"] is None:
                    continue
                signal[j] = True
        cnt = {e: 0 for e in self.ENGS}
        dcnt = {}
        tok = [None] * n
        for i, o in enumerate(ops):
            if o["dsem"] is not None:
                dcnt[o["slot"]] = dcnt.get(o["slot"], 0) + o["dinc"]
                tok[i] = (("d", o["slot"]), dcnt[o["slot"]])
            elif signal[i]:
                cnt[o["eng"]] += 1
                tok[i] = (("e", o["eng"]), cnt[o["eng"]])
        self.stats = dict(n=n, cnt=dict(cnt), ndsem=len(dcnt))
        with ExitStack() as st:
            sems = {}
            for e in self.ENGS:
                sems[("e", e)] = st.enter_context(nc.semaphore("s_" + e))
            for k in dcnt:
                sems[("d", k)] = st.enter_context(nc.semaphore("d_%d" % k))
            per = {e: [] for e in self.ENGS}
            for i, o in enumerate(ops):
                per[o["eng"]].append(i)
            block = st.enter_context(nc.Block())

            def run(ename, e):
                waited = {}
                for i in per[ename]:
                    o = ops[i]
                    need = {}
                    for j in o["deps"]:
                        t = tok[j]
                        if t is None:
                            continue
                        oj = ops[j]
                        if oj["dsem"] is None and oj["eng"] == ename and ename in ("pe", "sp") and o["dsem"] is None:
                            continue
                        if t[1] > need.get(t[0], 0):
                            need[t[0]] = t[1]
                    for sk, v in need.items():
                        if waited.get(sk, 0) >= v:
                            continue
                        waited[sk] = v
                        e.wait_ge(sems[sk], v)
                    ins = o["fn"](e)
                    t = tok[i]
                    if t is not None:
                        ins.then_inc(sems[t[0]], o["dinc"] if o["dsem"] is not None else 1)

            @block.tensor
            def _(e):
                run("pe", e)

            @block.scalar
            def _(e):
                run("act", e)

            @block.vector
            def _(e):
                run("dve", e)

            @block.gpsimd
            def _(e):
                run("pool", e)

            @block.sync
            def _(e):
                run("sp", e)


def make_consts():
    c = np.zeros((128, NCONS), np.float32)
    p = np.arange(128)[:, None]
    q = np.arange(128)[None, :]
    c[:, C_I:C_I + 128] = (p == q)
    c[:, C_NI:C_NI + 128] = -1.0 * (p == q)
    pm = np.zeros((128, 128), np.float32)
    for o in (0, 64):
        for d in range(8):
            pm[o + d + 8, o + d] = -1.0
            pm[o + d, o + d + 8] = 1.0
    c[:, C_PM:C_PM + 128] = pm
    c[:, C_SBM:C_SBM + 128] = np.where(q <= p, -BIG, 0.0)
    c[:, C_CAUS:C_CAUS + 128] = np.where(p <= q, 0.0, -BIG)
    c[:, C_SWAP:C_SWAP + 128] = np.where(p > q, 0.0, -BIG)
    c[:, C_DILP:C_DILP + 128] = np.where(p >= q, 0.0, -BIG)
    for n in range(32):
        c[n, C_ESEL + n * 128:C_ESEL + (n + 1) * 128] = 1.0
    return c


def rope_tabs(S):
    pos = np.arange(S, dtype=np.float32)
    inv = (np.float32(ROPE_THETA) ** (-np.arange(0, 16, 2, dtype=np.float32) / np.float32(16))).astype(np.float32)
    ang = (pos[:, None] * inv[None, :]).astype(np.float32)
    cs = np.cos(ang).astype(np.float32).T
    sn = np.sin(ang).astype(np.float32).T
    cos = np.ones((128, S), np.float32)
    sin = np.zeros((128, S), np.float32)
    for o in (0, 64):
        cos[o:o + 8] = cs
        cos[o + 8:o + 16] = cs
        sin[o:o + 8] = sn
        sin[o + 8:o + 16] = sn
    return cos, sin


def layer_cols(layer, hh):
    fm = []
    vg = []
    h0 = hh * 512
    ar = np.arange
    if layer in (0, 1):
        rope = layer == 1
        for t in range(4):
            fm.append(("q", ar(h0 + t * 128, h0 + t * 128 + 128), rope))
        for t in range(4):
            fm.append(("k", 1024 + ar(h0 + t * 128, h0 + t * 128 + 128), rope))
        for t in range(4):
            fm.append(("z", 3072 + ar(h0 + t * 128, h0 + t * 128 + 128), False))
        vg.append((2048 + ar(h0, h0 + 512), [(h, h % 2) for h in range(8)]))
    elif layer == 2:
        for t in range(4):
            fm.append(("q", ar(h0 + t * 128, h0 + t * 128 + 128), True))
        for g in range(2):
            kv = 2 * hh + g
            cc = 1024 + ar(kv * 64, kv * 64 + 64)
            fm.append(("k", np.concatenate([cc, cc]), True))
        for t in range(4):
            fm.append(("z", 1536 + ar(h0 + t * 128, h0 + t * 128 + 128), False))
        vg.append((1280 + ar(2 * hh * 64, 2 * hh * 64 + 128), [(0, 0), (0, 1), (1, 0), (1, 1)]))
    else:
        for g in range(3):
            for t in range(4):
                fm.append(("q", (3 * g) * 1024 + ar(h0 + t * 128, h0 + t * 128 + 128), True))
            for t in range(4):
                fm.append(("k", (3 * g + 1) * 1024 + ar(h0 + t * 128, h0 + t * 128 + 128), True))
        for t in range(4):
            fm.append(("z", 9216 + ar(h0 + t * 128, h0 + t * 128 + 128), False))
        for g in range(3):
            vg.append(((3 * g + 2) * 1024 + ar(h0, h0 + 512), [(h, h % 2) for h in range(8)]))
    return fm, vg


def layer_dims(layer):
    fm, vg = layer_cols(layer, 0)
    nfm = len(fm)
    nv = [len(v[0]) for v in vg]
    nvirt = [len(v[1]) for v in vg]
    return nfm, nv, nvirt


class Prog:
    def __init__(self, S):
        self.S = S
        self.nc = bass.Bass("TRN2", target_bir_lowering=False)
        self.sc = Sched(self.nc)
        nc = self.nc
        self.arena = nc.alloc_sbuf_tensor("arena", [128, ARENA_BYTES // 2], BF16)
        self.consb = nc.alloc_sbuf_tensor("consb", [128, NCONS], BF16)
        self.identf = nc.alloc_sbuf_tensor("identf", [128, 128], F32)
        self.ps = [nc.alloc_psum_tensor("ps%d" % i, [128, 512], F32) for i in range(8)]
        self.aoff = 0
        self.uid = 0
        self.phase = "init"
        self.after_hp = None

    def reset(self, phase):
        self.sc.barrier()
        self.aoff = 0
        self.phase = phase

    def alloc(self, n, dt, name):
        nb = n * esz(dt)
        off = self.aoff
        self.aoff += (nb + 63) // 64 * 64
        assert self.aoff <= ARENA_BYTES, (self.phase, name, self.aoff)
        v = self.arena[:, off // 2:(off + nb) // 2]
        if dt == F32:
            v = v.bitcast(F32)
        self.uid += 1
        return v, "%s:%s:%d" % (self.phase, name, self.uid)

    def allocn(self, k, n, dt, name):
        return [self.alloc(n, dt, name + str(i)) for i in range(k)]

    def dram(self, name, shape, dt, kind="Internal"):
        return self.nc.dram_tensor(name, list(shape), dt, kind=kind).ap()

    def dma(self, out, in_, r, w, dsem, q="sp"):
        self.sc.op(q, lambda e: e.dma_start(out=out, in_=in_), r=r, w=w, dsem=dsem)

    def mm(self, out, lhsT, rhs, start, stop, r, w):
        self.sc.op("pe", lambda e: e.matmul(out, lhsT, rhs, start=start, stop=stop), r=r, w=w)

    def tr(self, out, in_, ident, r, w):
        self.sc.op("pe", lambda e: e.transpose(out, in_, ident), r=r, w=w)

    def act(self, out, in_, func, r, w, scale=1.0, bias=0.0):
        self.sc.op("act", lambda e: e.activation(out, in_, func, bias=bias, scale=scale), r=r, w=w)

    def copy(self, eng, out, in_, r, w):
        if eng == "act":
            self.sc.op("act", lambda e: e.copy(out, in_), r=r, w=w)
        else:
            self.sc.op(eng, lambda e: e.tensor_copy(out, in_), r=r, w=w)

    def tt(self, eng, out, a, b, op, r, w):
        self.sc.op(eng, lambda e: e.tensor_tensor(out, a, b, op), r=r, w=w)

    def memset(self, eng, ap, val, w):
        self.sc.op(eng, lambda e: e.memset(ap, val), r=(), w=w)

    def load_consts(self, cons_d):
        self.reset("cons")
        stg, k = self.alloc(NCONS, F32, "cstg")
        self.dma(stg, cons_d, r=[], w=[k], dsem="cstg")
        self.copy("dve", self.consb[:, :], stg, r=[k], w=["consb"])
        self.copy("act", self.identf[:, :], stg[:, C_I:C_I + 128], r=[k], w=["identf"])

    def cb(self, off, n=128, rows=128):
        return self.consb[0:rows, off:off + n]

    def p1(self, layer, x_d, xkey, w_d, cos_d, sin_d, fm_d, vva_ds, lname):
        S = self.S
        fm, vg = layer_cols(layer, 0)
        nfm = len(fm)
        Fc = nfm * 128 + sum(len(v[0]) for v in vg)
        self.reset("p1_%d" % layer)
        W, kW = self.alloc(8 * Fc, BF16, "W")
        W3 = W.rearrange("p (c f) -> p c f", c=8)
        wst = self.allocn(2, 2048, F32, "wst")
        xin = self.allocn(2, 4 * 1024, F32, "xin")
        xT = self.allocn(2, 8 * 512, BF16, "xT")
        rope_any = any(f[2] for f in fm)
        if rope_any:
            cst = self.allocn(2, 512, F32, "cos")
            sst = self.allocn(2, 512, F32, "sin")
            qb = self.allocn(2, 512, BF16, "qb")
            t1 = self.allocn(2, 512, F32, "t1")
            t2 = self.allocn(2, 512, F32, "t2")
        stg = self.allocn(3, 512, BF16, "stg")
        vst = self.allocn(3, 1024, BF16, "vst")
        if layer == 1:
            kms, kkms = self.alloc(4 * 32, F32, "kms")
            kms3 = kms.rearrange("p (a n) -> p a n", a=4)
        for (v, k) in vst:
            self.memset("pool", v, 1.0, w=[k])
        wv = w_d.rearrange("(c p) f -> p c f", p=128)
        pi = 0
        for c in range(8):
            for f0 in range(0, Fc, 2048):
                fw = min(2048, Fc - f0)
                s_, ks = wst[pi % 2]
                self.dma(s_[:, 0:fw], wv[:, c, f0:f0 + fw], r=[], w=[ks], dsem=ks)
                self.copy("dve" if pi % 2 == 0 else "pool", W3[:, c, f0:f0 + fw], s_[:, 0:fw], r=[ks], w=[kW])
                pi += 1
        ntt = S // 512
        si = 0
        vi = 0
        ri = 0
        psi = 0
        for tt in range(ntt):
            xi, kxi = xin[tt % 2]
            xi3 = xi.rearrange("p (s c) -> p s c", s=4)
            self.dma(xi3, x_d[tt * 512:(tt + 1) * 512, :].rearrange("(s p) c -> p s c", p=128),
                     r=[xkey], w=[kxi], dsem=kxi)
            xt, kxt = xT[tt % 2]
            xt3 = xt.rearrange("p (c t) -> p c t", c=8)
            for c in range(8):
                pb = self.ps[psi % 2]
                kp = ("ps", psi % 2)
                psi += 1
                for s in range(4):
                    self.tr(pb[:, s * 128:(s + 1) * 128], xi3[:, s, c * 128:(c + 1) * 128], self.identf[:, :],
                            r=[kxi, "identf"], w=[kp])
                self.copy("act" if c % 2 == 0 else "dve", xt3[:, c, :], pb[:, :], r=[kp], w=[kxt])
            if rope_any:
                cs_, kcs = cst[tt % 2]
                sn_, ksn = sst[tt % 2]
                self.dma(cs_, cos_d[:, tt * 512:(tt + 1) * 512], r=[], w=[kcs], dsem=kcs)
                self.dma(sn_, sin_d[:, tt * 512:(tt + 1) * 512], r=[], w=[ksn], dsem=ksn)
            for fi, (kind, _, rope) in enumerate(fm):
                b = 2 + (fi % 3)
                pb = self.ps[b]
                kp = ("ps", b)
                for c in range(8):
                    self.mm(pb[:, :], W3[:, c, fi * 128:(fi + 1) * 128], xt3[:, c, :], start=(c == 0), stop=(c == 7),
                            r=[kW, kxt], w=[kp])
                so, kso = stg[si % 3]
                si += 1
                sc_ = 0.125 if kind == "q" else 1.0
                if kind == "z":
                    self.act(so, pb[:, :], AF.Silu, r=[kp], w=[kso])
                elif not rope:
                    self.act(so, pb[:, :], AF.Copy, r=[kp], w=[kso], scale=sc_)
                else:
                    qb_, kqb = qb[ri % 2]
                    t1_, kt1 = t1[ri % 2]
                    t2_, kt2 = t2[ri % 2]
                    ri += 1
                    self.act(qb_, pb[:, :], AF.Copy, r=[kp], w=[kqb], scale=sc_)
                    pr = self.ps[5 + (ri % 2)]
                    kpr = ("ps", 5 + (ri % 2))
                    self.mm(pr[:, :], self.cb(C_PM), qb_, start=True, stop=True, r=["consb", kqb], w=[kpr])
                    self.tt("dve", t1_, pr[:, :], sn_, ALU.mult, r=[kpr, ksn], w=[kt1])
                    self.sc.op("dve", (lambda e, o=t2_, i0=pb[:, :], s=sc_, i1=cs_:
                                       e.scalar_tensor_tensor(o, i0, s, i1, ALU.mult, ALU.mult)),
                               r=[kp, kcs], w=[kt2])
                    self.tt("pool", so, t1_, t2_, ALU.add, r=[kt1, kt2], w=[kso])
                    if layer == 1 and kind == "k":
                        kt_i = fi - 4
                        self.sc.op("dve", (lambda e, o=kms3[:, kt_i, 2 * tt:2 * tt + 2],
                                           i=so.rearrange("p (a b) -> p a b", a=2):
                                           e.tensor_reduce(o, i, AX.X, ALU.add)),
                                   r=[kso], w=[kkms])
                self.dma(fm_d[fi * 128:(fi + 1) * 128, tt * 512:(tt + 1) * 512], so, r=[kso],
                         w=["D:fm" + lname], dsem=kso)
            voff = nfm * 128
            for gi, (vc, virt) in enumerate(vg):
                nvc = len(vc)
                for s in range(4):
                    b = 2 + ((gi * 4 + s) % 3)
                    pb = self.ps[b]
                    kp = ("ps", b)
                    for c in range(8):
                        self.mm(pb[:, 0:nvc], xt3[:, c, s * 128:(s + 1) * 128], W3[:, c, voff:voff + nvc],
                                start=(c == 0), stop=(c == 7), r=[kW, kxt], w=[kp])
                    vs, kvs = vst[vi % 3]
                    vi += 1
                    nvirt = len(virt)
                    vs3 = vs[:, 0:nvirt * 128].rearrange("p (h e) -> p h e", e=128)
                    for form in (0, 1):
                        idxs = [i for i, (sh, fo) in enumerate(virt) if fo == form]
                        if not idxs:
                            continue
                        i0 = idxs[0]
                        st_i = (idxs[1] - idxs[0]) if len(idxs) > 1 else 1
                        s0 = virt[i0][0]
                        st_s = (virt[idxs[1]][0] - s0) if len(idxs) > 1 else 1
                        n_i = len(idxs)
                        out_ap = vs3[:, i0:i0 + st_i * (n_i - 1) + 1:st_i, form * 64:form * 64 + 64]
                        src = pb[:, 0:nvc].rearrange("p (h e) -> p h e", e=64)
                        in_ap = src[:, s0:s0 + st_s * (n_i - 1) + 1:st_s, :]
                        self.copy("act" if form == 0 else "dve", out_ap, in_ap, r=[kp], w=[kvs])
                    t0 = tt * 512 + s * 128
                    self.dma(vva_ds[gi][t0:t0 + 128, :], vs[:, 0:nvirt * 128], r=[kvs],
                             w=["D:vva" + lname], dsem=kvs)
                voff += nvc
        if layer == 1:
            return kms3, kkms
        return None

    def finalize(self, src, ksrc, base, cols, SZ, kSZ, YG, kYG, q0, rec2, yt2, fi, es=None):
        nb, db = base, 64 - base
        rc, krc = rec2[fi % 2]
        yt, kyt = yt2[fi % 2]
        if es is not None:
            es_ap, kes = es
            self.sc.op("dve", (lambda e, o=rc[nb:nb + 64, 0:cols], i=src[db:db + 64, 0:cols], s=es_ap[nb:nb + 64, :]:
                               e.tensor_scalar(o, i, s, None, ALU.add)), r=[ksrc, kes], w=[krc])
            self.sc.op("dve", lambda e, o=rc[nb:nb + 64, 0:cols]: e.reciprocal(o, o), r=[krc], w=[krc])
        else:
            self.sc.op("dve", lambda e, o=rc[nb:nb + 64, 0:cols], i=src[db:db + 64, 0:cols]: e.reciprocal(o, i),
                       r=[ksrc], w=[krc])
        self.tt("dve", yt[nb:nb + 64, 0:cols], src[nb:nb + 64, 0:cols], rc[nb:nb + 64, 0:cols], ALU.mult,
                r=[ksrc, krc], w=[kyt])
        self.tt("pool", YG[nb:nb + 64, q0:q0 + cols], yt[nb:nb + 64, 0:cols], SZ[nb:nb + 64, q0:q0 + cols], ALU.mult,
                r=[kyt, kSZ], w=[kYG])

    @staticmethod
    def pipeline(units, nst):
        n = len(units)
        for slot in range(n + nst - 1):
            for k in range(nst):
                i = slot - k
                if 0 <= i < n:
                    for th in units[i][k]:
                        th()

    def load_pair(self, fm_d, rows, lname, QT, kQT, KT, kKT, SZ, kSZ):
        qr, kr, zr = rows
        self.dma(QT, fm_d[qr * 128:(qr + 1) * 128, :], r=["D:fm" + lname], w=[kQT], dsem=kQT)
        if kr is not None:
            self.dma(KT, fm_d[kr * 128:(kr + 1) * 128, :], r=["D:fm" + lname], w=[kKT], dsem=kKT)
        if zr is not None:
            self.dma(SZ, fm_d[zr * 128:(zr + 1) * 128, :], r=["D:fm" + lname], w=[kSZ], dsem=kSZ)

    def load_va(self, VA4, kVA, src, lname):
        self.dma(VA4, src.rearrange("(k p) (f e) -> p k f e", p=128, e=128), r=["D:vva" + lname], w=[kVA], dsem=kVA)

    def p2_sb(self, fm_d, vva_d, yg_d, lname):
        S = self.S
        nkb = S // 128
        self.reset("p2sb")
        QT, kQT = self.alloc(S, BF16, "QT")
        KT, kKT = self.alloc(S, BF16, "KT")
        SZ, kSZ = self.alloc(S, BF16, "SZ")
        YG, kYG = self.alloc(S, BF16, "YG")
        VA, kVA = self.alloc(nkb * 256, BF16, "VA")
        VA4 = VA.rearrange("p (k f e) -> p k f e", f=2, e=128)
        ones, kon = self.alloc(512, F32, "ones")
        U = self.allocn(3, 512, F32, "U")
        SX = self.allocn(3, 528, BF16, "SX")
        AT = self.allocn(3, 512, BF16, "AT")
        self.memset("pool", ones, 1.0, w=[kon])
        for hp in range(4):
            self.load_pair(fm_d, (hp, 4 + hp, 8 + hp), lname, QT, kQT, KT, kKT, SZ, kSZ)
            self.load_va(VA4, kVA, vva_d[:, hp * 256:(hp + 1) * 256], lname)
            units = []
            t = 0
            oti = 0
            for hl in range(2):
                base = 64 * hl
                for qb in range(nkb):
                    c0 = qb * 128
                    nch = (S - c0 + 511) // 512
                    ot = self.ps[5 + oti % 2]
                    kot = ("ps", 5 + oti % 2)
                    oti += 1
                    for j in range(nch):
                        k0 = c0 + 512 * j
                        wd = min(512, S - k0)
                        nm = wd // 128
                        zb, kz = self.ps[t % 3], ("ps", t % 3)
                        tb, ktb = self.ps[3 + t % 2], ("ps", 3 + t % 2)
                        u_, ku = U[t % 3]
                        sx, ksx = SX[t % 3]
                        psx, kpsx = SX[(t - 1) % 3]
                        at, kat = AT[t % 3]
                        a, b, c = [], [], []

                        def s_qk(zb=zb, kz=kz, base=base, c0=c0, k0=k0, wd=wd, j=j):
                            self.mm(zb[:, 0:wd], QT[base:base + 64, c0:c0 + 128], KT[base:base + 64, k0:k0 + wd],
                                    start=True, stop=(j != 0), r=[kQT, kKT], w=[kz])
                            if j == 0:
                                self.mm(zb[:, 0:128], self.cb(C_I), self.cb(C_SBM), start=False, stop=True,
                                        r=["consb"], w=[kz])

                        def s_sig(zb=zb, kz=kz, u_=u_, ku=ku, wd=wd):
                            self.act(u_[:, 0:wd], zb[:, 0:wd], AF.Sigmoid, r=[kz], w=[ku], scale=-1.0)

                        def s_scan(u_=u_, ku=ku, sx=sx, ksx=ksx, psx=psx, kpsx=kpsx, wd=wd, j=j):
                            if j == 0:
                                self.memset("pool", sx[:, 8:9], 1.0, w=[ksx])
                                ini = 1.0
                                rr = [ku, kon, ksx]
                            else:
                                self.copy("pool", sx[:, 8:9], psx[:, 520:521], r=[kpsx], w=[ksx])
                                ini = psx[:, 520:521]
                                rr = [ku, kon, ksx, kpsx]
                            self.sc.op("dve", (lambda e, o=sx[:, 9:9 + wd], d0=u_[:, 0:wd], d1=ones[:, 0:wd], ini=ini:
                                               e.tensor_tensor_scan(o, d0, d1, ini, ALU.mult, ALU.mult)),
                                       r=rr, w=[ksx])

                        def s_tr(sx=sx, ksx=ksx, tb=tb, ktb=ktb, nm=nm):
                            for m in range(nm):
                                self.mm(tb[:, m * 128:(m + 1) * 128], sx[:, 8 + m * 128:8 + (m + 1) * 128],
                                        self.cb(C_I), start=True, stop=False, r=[ksx, "consb"], w=[ktb])
                                self.mm(tb[:, m * 128:(m + 1) * 128], sx[:, 9 + m * 128:9 + (m + 1) * 128],
                                        self.cb(C_NI), start=False, stop=True, r=[ksx, "consb"], w=[ktb])

                        def s_ev(tb=tb, ktb=ktb, at=at, kat=kat, wd=wd, t=t):
                            self.copy("act" if t % 2 == 0 else "dve", at[:, 0:wd], tb[:, 0:wd], r=[ktb], w=[kat])

                        def s_av(at=at, kat=kat, ot=ot, kot=kot, k0=k0, nm=nm, j=j, nch=nch, hl=hl, base=base, c0=c0):
                            for m in range(nm):
                                kb = (k0 // 128) + m
                                self.mm(ot[:, 0:128], VA4[:, kb, hl, :], at[:, m * 128:(m + 1) * 128],
                                        start=(j == 0 and m == 0), stop=(j == nch - 1 and m == nm - 1),
                                        r=[kVA, kat], w=[kot])
                            if j == nch - 1:
                                self.tt("dve", YG[base:base + 64, c0:c0 + 128], ot[base:base + 64, 0:128],
                                        SZ[base:base + 64, c0:c0 + 128], ALU.mult, r=[kot, kSZ], w=[kYG])
                        units.append(([s_qk, s_sig, s_scan], [s_tr, s_ev], [s_av]))
                        t += 1
            self.pipeline(units, 3)
            self.dma(yg_d[hp * 128:(hp + 1) * 128, :], YG, r=[kYG], w=["D:yg" + lname], dsem=kYG)
            if self.after_hp is not None:
                self.after_hp(hp)

    def p2_moba(self, fm_d, vva_d, yg_d, kms3, kkms, lname):
        S = self.S
        nkb = S // 128
        nblk = S // 256
        self.sc.barrier()
        kmb_t = self.nc.alloc_sbuf_tensor("kmb", [128, 4 * 32], BF16)
        kmb = kmb_t[:, :].rearrange("p (a n) -> p a n", a=4)
        self.copy("dve", kmb, kms3, r=[kkms], w=["kmb"])
        self.reset("p2moba")
        QT, kQT = self.alloc(S, BF16, "QT")
        KT, kKT = self.alloc(S, BF16, "KT")
        SZ, kSZ = self.alloc(S, BF16, "SZ")
        YG, kYG = self.alloc(S, BF16, "YG")
        VA, kVA = self.alloc(nkb * 256, BF16, "VA")
        VA4 = VA.rearrange("p (k f e) -> p k f e", f=2, e=128)
        P = self.allocn(3, 512, BF16, "P")
        Gs, kGs = self.alloc(4 * 32, F32, "Gs")
        Gs3 = Gs.rearrange("p (a n) -> p a n", a=4)
        M8, kM8 = self.alloc(4 * 8, F32, "M8")
        M83 = M8.rearrange("p (a n) -> p a n", a=4)
        NMt, kNMt = self.alloc(4 * 32, F32, "NMt")
        NMt3 = NMt.rearrange("p (a n) -> p a n", a=4)
        NM = self.allocn(2, 512, BF16, "NM")
        rec2 = self.allocn(2, 512, F32, "rec")
        yt2 = self.allocn(2, 512, F32, "yt")

        def nm_build(base, hp, qt, nm_, knm):
            gp, kgp = self.ps[5], ("ps", 5)
            gp3 = gp[:, 0:128].rearrange("p (a n) -> p a n", a=4)
            for sub in range(4):
                tb = 4 * qt + sub
                self.mm(gp3[:, sub, 0:nblk], QT[base:base + 64, tb * 128:(tb + 1) * 128],
                        kmb[base:base + 64, hp, 0:nblk], start=True, stop=True, r=[kQT, "kmb"], w=[kgp])
            self.memset("pool", Gs, -1.0e30, w=[kGs])
            for sub in range(4):
                qblk = (4 * qt + sub) // 2
                if qblk > 0:
                    self.copy("dve", Gs3[:, sub, 0:qblk], gp3[:, sub, 0:qblk], r=[kgp], w=[kGs])
            for sub in range(4):
                self.sc.op("dve", lambda e, o=M83[:, sub, :], i=Gs3[:, sub, :]: e.max(o, i), r=[kGs], w=[kM8])
            for sub in range(4):
                self.sc.op("dve", (lambda e, o=NMt3[:, sub, :], i=Gs3[:, sub, :], s=M83[:, sub, 2:3]:
                                   e.tensor_scalar(o, i, s, -BIG, ALU.is_lt, ALU.mult)),
                           r=[kGs, kM8], w=[kNMt])
            for sub in range(4):
                qblk = (4 * qt + sub) // 2
                self.memset("pool", NMt3[:, sub, qblk:qblk + 1], 0.0, w=[kNMt])
                if qblk + 1 < 32:
                    self.memset("pool", NMt3[:, sub, qblk + 1:32], -BIG, w=[kNMt])
            np_, knp = self.ps[6], ("ps", 6)
            for sub in range(4):
                self.tr(np_[0:32, sub * 128:(sub + 1) * 128], NMt3[:, sub, :], self.identf[:, :],
                        r=[kNMt, "identf"], w=[knp])
            self.copy("act", nm_[0:32, :], np_[0:32, :], r=[knp], w=[knm])

        for hp in range(4):
            self.load_pair(fm_d, (hp, 4 + hp, 8 + hp), lname, QT, kQT, KT, kKT, SZ, kSZ)
            self.load_va(VA4, kVA, vva_d[:, hp * 256:(hp + 1) * 256], lname)
            tiles = [(hl, qt) for hl in range(2) for qt in range(S // 512)]
            units = []
            t = 0
            for ti, (hl, qt) in enumerate(tiles):
                base = 64 * hl
                q0 = qt * 512
                nm_, knm = NM[ti % 2]
                ot, kot = self.ps[3 + ti % 2], ("ps", 3 + ti % 2)
                nkeys = 4 * qt + 4
                for kb in range(nkeys):
                    j = kb - 4 * qt
                    c0 = 128 * j if j > 0 else 0
                    n = kb // 2
                    sb_, ksb = self.ps[t % 3], ("ps", t % 3)
                    p_, kp_ = P[t % 3]
                    a, b = [], []
                    if kb == 0:
                        if ti == 0:
                            a.append(lambda base=base, hp=hp, qt=qt, nm_=nm_, knm=knm: nm_build(base, hp, qt, nm_, knm))
                        if ti + 1 < len(tiles):
                            hl2, qt2 = tiles[ti + 1]
                            nm2, knm2 = NM[(ti + 1) % 2]
                            a.append(lambda base=64 * hl2, hp=hp, qt=qt2, nm_=nm2, knm=knm2:
                                     nm_build(base, hp, qt, nm_, knm))

                    def s_qk(sb_=sb_, ksb=ksb, p_=p_, kp_=kp_, base=base, kb=kb, q0=q0, c0=c0, n=n, j=j, nm_=nm_,
                             knm=knm):
                        self.mm(sb_[:, c0:512], KT[base:base + 64, kb * 128:(kb + 1) * 128],
                                QT[base:base + 64, q0 + c0:q0 + 512], start=True, stop=False, r=[kKT, kQT], w=[ksb])
                        self.mm(sb_[:, c0:512], self.cb(C_ESEL + n * 128, 128, 32), nm_[0:32, c0:512],
                                start=False, stop=(j < 0), r=["consb", knm], w=[ksb])
                        if j >= 0:
                            self.mm(sb_[:, c0:c0 + 128], self.cb(C_I), self.cb(C_CAUS), start=False, stop=True,
                                    r=["consb"], w=[ksb])
                        self.act(p_[:, c0:512], sb_[:, c0:512], AF.Exp, r=[ksb], w=[kp_])

                    def s_av(p_=p_, kp_=kp_, ot=ot, kot=kot, kb=kb, hl=hl, c0=c0, nkeys=nkeys, base=base, q0=q0, ti=ti):
                        self.mm(ot[:, c0:512], VA4[:, kb, hl, :], p_[:, c0:512], start=(kb == 0),
                                stop=(kb == nkeys - 1), r=[kVA, kp_], w=[kot])
                        if kb == nkeys - 1:
                            self.finalize(ot, kot, base, 512, SZ, kSZ, YG, kYG, q0, rec2, yt2, ti)
                    a.append(s_qk)
                    b.append(s_av)
                    units.append((a, b))
                    t += 1
            self.pipeline(units, 2)
            self.dma(yg_d[hp * 128:(hp + 1) * 128, :], YG, r=[kYG], w=["D:yg" + lname], dsem=kYG)
            if self.after_hp is not None:
                self.after_hp(hp)

    def band_unit(self, u, base, hl, QT, kQT, KT, kKT, VA4, kVA, P, blocks, prev_mask, ot, kot, tail):
        kvas = list(kVA) if isinstance(kVA, list) else [kVA]
        banks = []
        for which in (0, 1):
            bi = (2 * u + which) % 4
            banks.append((self.ps[bi], ("ps", bi), P[bi][0], P[bi][1]))

        def s_a():
            for which in (0, 1):
                sb_, ksb, p_, kp_ = banks[which]
                jl = [jb for jb in range(4) if blocks[jb][1 + which] is not None]
                for jb in jl:
                    (qs, qstep) = blocks[jb][0]
                    (ks, kstep), _ = blocks[jb][1 + which]
                    self.mm(sb_[:, jb * 128:(jb + 1) * 128],
                            KT[base:base + 64, ks:ks + kstep * 127 + 1:kstep],
                            QT[base:base + 64, qs:qs + qstep * 127 + 1:qstep], start=True, stop=False,
                            r=[kKT, kQT], w=[ksb])
                    self.mm(sb_[:, jb * 128:(jb + 1) * 128], self.cb(C_I),
                            self.cb(prev_mask if which == 0 else C_CAUS), start=False, stop=True,
                            r=["consb"], w=[ksb])
                runs = []
                for jb in jl:
                    if runs and runs[-1][1] == jb:
                        runs[-1][1] = jb + 1
                    else:
                        runs.append([jb, jb + 1])
                for a_, b_ in runs:
                    self.act(p_[:, a_ * 128:b_ * 128], sb_[:, a_ * 128:b_ * 128], AF.Exp, r=[ksb], w=[kp_])

        def s_b():
            for jb in range(4):
                first = True
                for which in (0, 1):
                    ent = blocks[jb][1 + which]
                    if ent is None:
                        continue
                    _, vidx = ent
                    sb_, ksb, p_, kp_ = banks[which]
                    self.mm(ot[:, jb * 128:(jb + 1) * 128], VA4[:, vidx, hl, :], p_[:, jb * 128:(jb + 1) * 128],
                            start=first, stop=(which == 1), r=kvas + [kp_], w=[kot])
                    first = False
            tail()
        return ([s_a], [s_b])

    def p2_swa(self, fm_d, vva_d, yg_d, es_d, hh, lname):
        S = self.S
        nkb = S // 128
        self.reset("p2swa")
        QT, kQT = self.alloc(S, BF16, "QT")
        KT, kKT = self.alloc(S, BF16, "KT")
        SZ, kSZ = self.alloc(S, BF16, "SZ")
        YG, kYG = self.alloc(S, BF16, "YG")
        VA, kVA = self.alloc(nkb * 256, BF16, "VA")
        VA4 = VA.rearrange("p (k f e) -> p k f e", f=2, e=128)
        P = self.allocn(4, 512, BF16, "P")
        ES, kES = self.alloc(16, F32, "ES")
        rec2 = self.allocn(2, 512, F32, "rec")
        yt2 = self.allocn(2, 512, F32, "yt")
        self.dma(ES, es_d, r=[], w=[kES], dsem=kES)
        self.act(ES, ES, AF.Exp, r=[kES], w=[kES])
        u = 0
        for hp in range(4):
            g = hp // 2
            self.load_pair(fm_d, (hp, (4 + g) if hp % 2 == 0 else None, 6 + hp), lname, QT, kQT, KT, kKT, SZ, kSZ)
            if hp % 2 == 0:
                self.load_va(VA4, kVA, vva_d[:, g * 256:(g + 1) * 256], lname)
            units = []
            for hl in range(2):
                base = 64 * hl
                hloc = hp * 2 + hl
                for qt in range(S // 512):
                    q0 = qt * 512
                    ot, kot = self.ps[4 + u % 2], ("ps", 4 + u % 2)
                    blocks = []
                    for jb in range(4):
                        qb = 4 * qt + jb
                        prev = ((((qb - 1) * 128, 1), qb - 1) if qb >= 1 else None)
                        blocks.append(((qb * 128, 1), prev, ((qb * 128, 1), qb)))

                    def tail(ot=ot, kot=kot, base=base, q0=q0, u=u, hloc=hloc):
                        self.finalize(ot, kot, base, 512, SZ, kSZ, YG, kYG, q0, rec2, yt2, u,
                                      es=(ES[:, hloc:hloc + 1], kES))
                    units.append(self.band_unit(u, base, hl, QT, kQT, KT, kKT, VA4, kVA, P, blocks, C_SWAP, ot, kot,
                                                tail))
                    u += 1
            self.pipeline(units, 2)
            self.dma(yg_d[hp * 128:(hp + 1) * 128, :], YG, r=[kYG], w=["D:yg" + lname], dsem=kYG)
            if self.after_hp is not None:
                self.after_hp(hp)

    def p2_dil(self, fm_d, vva_ds, yg_d, lname):
        S = self.S
        nkb = S // 128
        self.reset("p2dil")
        QT, kQT = self.alloc(S, BF16, "QT")
        KT, kKT = self.alloc(S, BF16, "KT")
        SZ, kSZ = self.alloc(S, BF16, "SZ")
        YG, kYG = self.alloc(S, BF16, "YG")
        VA, kVA = self.alloc(nkb * 256, BF16, "VA")
        VA4 = VA.rearrange("p (k f e) -> p k f e", f=2, e=128)
        ACC = self.allocn(2, S, F32, "ACC")
        P = self.allocn(4, 512, BF16, "P")
        rec2 = self.allocn(2, 512, F32, "rec")
        yt2 = self.allocn(2, 512, F32, "yt")
        u = 0
        fi = 0
        nspan = S // 2048
        for hp in range(4):
            self.dma(SZ, fm_d[(24 + hp) * 128:(25 + hp) * 128, :], r=["D:fm" + lname], w=[kSZ], dsem=kSZ)
            for g, (win, d) in enumerate(DIL):
                L = S // d
                nI = L // 128
                self.sc.barrier()
                self.load_pair(fm_d, (8 * g + hp, 8 * g + 4 + hp, None), lname, QT, kQT, KT, kKT, SZ, kSZ)
                for r_ in range(d):
                    src = vva_ds[g][r_:S:d, hp * 256:(hp + 1) * 256] if d > 1 else vva_ds[g][:, hp * 256:(hp + 1) * 256]
                    self.dma(VA4[:, r_ * nI:(r_ + 1) * nI, :, :],
                             src.rearrange("(k p) (f e) -> p k f e", p=128, e=128),
                             r=["D:vva" + lname], w=[(kVA, r_)], dsem=(kVA, r_ % 4))
                kva_all = [(kVA, r_) for r_ in range(d)]
                bps = 2048 // d // 128
                units = []
                for hl in range(2):
                    base = 64 * hl
                    acc, kacc = ACC[hl]
                    for sp_ in range(nspan):
                        s0 = sp_ * 2048
                        for bk in range(4):
                            ot, kot = self.ps[4 + u % 2], ("ps", 4 + u % 2)
                            blocks = []
                            for jb in range(4):
                                qi = 4 * bk + jb
                                r_ = qi // bps
                                I = sp_ * bps + (qi % bps)
                                tq = r_ + d * 128 * I
                                prev = (((r_ + d * 128 * (I - 1), d), r_ * nI + I - 1) if I >= 1 else None)
                                blocks.append(((tq, d), prev, ((tq, d), r_ * nI + I)))
                            span = acc[:, s0:s0 + 2048]
                            if d == 1:
                                dst = span[:, bk * 512:(bk + 1) * 512]
                                srcp = ot[:, :]
                            elif d == 4:
                                dst = span.rearrange("p (l r) -> p r l", r=4)[:, bk, :]
                                srcp = ot[:, :]
                            else:
                                dst = span.rearrange("p (l r) -> p r l", r=16)[:, 4 * bk:4 * bk + 4, :]
                                srcp = ot[:, :].rearrange("p (j l) -> p j l", j=4)

                            def tail(dst=dst, srcp=srcp, kot=kot, kacc=kacc, g=g):
                                if g == 0:
                                    self.copy("dve", dst, srcp, r=[kot], w=[kacc])
                                else:
                                    self.tt("dve", dst, dst, srcp, ALU.add, r=[kot, kacc], w=[kacc])
                            units.append(self.band_unit(u, base, hl, QT, kQT, KT, kKT, VA4, kva_all, P, blocks, C_DILP,
                                                        ot, kot, tail))
                            u += 1
                self.pipeline(units, 2)
            for hl in range(2):
                base = 64 * hl
                acc, kacc = ACC[hl]
                for qt in range(S // 512):
                    self.finalize(acc[:, qt * 512:(qt + 1) * 512], kacc, base, 512, SZ, kSZ, YG, kYG, qt * 512,
                                  rec2, yt2, fi)
                    fi += 1
            self.dma(yg_d[hp * 128:(hp + 1) * 128, :], YG, r=[kYG], w=["D:yg" + lname], dsem=kYG)
            if self.after_hp is not None:
                self.after_hp(hp)

    def p3(self, T, x_d, xkey, yg_parts, ygkeys, wout_d, g_d, b_d, out_d, outkey, rev_out, lname, x_nat=None, perm=False):
        self.reset("p3" + lname)
        WO, kWO = self.alloc(8 * 1024, BF16, "WO")
        WO3 = WO.rearrange("p (c f) -> p c f", c=8)
        wst = self.allocn(2, 2048, F32, "wst")
        G, kG = self.alloc(1024, F32, "G")
        Bt, kBt = self.alloc(1024, F32, "Bt")
        yg = self.allocn(2, 8 * 512, BF16, "yg")
        ygr = self.allocn(2, 8 * 512, BF16, "ygr") if rev_out else None
        xin = self.allocn(2, 4 * 1024, F32, "xin")
        R = self.allocn(2, 1024, F32, "R")
        XN = self.allocn(2, 1024, F32, "XN")
        O = self.allocn(2, 1024, F32, "O")
        ST = self.allocn(2, 16, F32, "ST")
        MV = self.allocn(2, 4, F32, "MV")
        wv = wout_d.rearrange("(c p) f -> p c f", p=128)
        pi = 0
        for c in range(8):
            s_, ks = wst[pi % 2]
            cdst = ((c % 4) * 2 + c // 4) if perm else c
            self.dma(s_[:, 0:1024], wv[:, c, :], r=[], w=[ks], dsem=ks)
            self.copy("dve" if pi % 2 == 0 else "pool", WO3[:, cdst, :], s_[:, 0:1024], r=[ks], w=[kWO])
            pi += 1
        self.dma(G, g_d, r=[], w=[kG], dsem=kG)
        self.dma(Bt, b_d, r=[], w=[kBt], dsem=kBt)
        it = 0
        for tt in range(T // 512):
            y_, ky = yg[tt % 2]
            y3 = y_.rearrange("p (c t) -> p c t", c=8)
            for half in range(2):
                self.dma(y3[:, 4 * half:4 * half + 4, :],
                         yg_parts[half][:, tt * 512:(tt + 1) * 512].rearrange("(c p) t -> p c t", p=128),
                         r=[ygkeys[half]], w=[ky + "h%d" % half], dsem=ky + "h%d" % half)
            xi, kxi = xin[tt % 2]
            xi3 = xi.rearrange("p (s c) -> p s c", s=4)
            if rev_out:
                row0 = T - (tt + 1) * 512
                yr_, kyr = ygr[tt % 2]
                yr3 = yr_.rearrange("p (c t) -> p c t", c=8)
                self.copy("dve", yr3, y3[:, :, ::-1], r=[ky + "h0", ky + "h1"], w=[kyr])
                y3 = yr3
                ykeys = [kyr]
                xsrc = x_nat
            else:
                row0 = tt * 512
                ykeys = [ky + "h0", ky + "h1"]
                xsrc = x_d
            self.dma(xi3, xsrc[row0:row0 + 512, :].rearrange("(s p) c -> p s c", p=128),
                     r=[xkey], w=[kxi], dsem=kxi)
            for s in range(4):
                r_, kr = R[it % 2]
                xn, kxn = XN[it % 2]
                o_, ko = O[it % 2]
                st_, kst = ST[it % 2]
                mv, kmv = MV[it % 2]
                st3 = st_[:, 0:12].rearrange("p (a b) -> p a b", a=2)
                for hf in range(2):
                    b = (2 * it + hf) % 4
                    pb = self.ps[b]
                    kp = ("ps", b)
                    for c in range(8):
                        self.mm(pb[:, :], y3[:, c, s * 128:(s + 1) * 128], WO3[:, c, hf * 512:(hf + 1) * 512],
                                start=(c == 0), stop=(c == 7), r=ykeys + [kWO], w=[kp])
                    self.sc.op("dve", (lambda e, o=r_[:, hf * 512:(hf + 1) * 512], i0=xi3[:, s, hf * 512:(hf + 1) * 512],
                                       i1=pb[:, :]: e.scalar_tensor_tensor(o, i0, float(ALPHA), i1, ALU.mult, ALU.add)),
                               r=[kxi, kp], w=[kr + "h%d" % hf])
                    self.sc.op("dve", lambda e, o=st3[:, hf, :], i=r_[:, hf * 512:(hf + 1) * 512]: e.bn_stats(o, i),
                               r=[kr + "h%d" % hf], w=[kst + "h%d" % hf])
                self.sc.op("dve", lambda e, o=mv[:, 0:2], i=st3: e.bn_aggr(o, i), r=[kst + "h0", kst + "h1"], w=[kmv])
                self.sc.op("dve", (lambda e, o=mv[:, 2:3], i=mv[:, 1:2]:
                                   e.tensor_scalar(o, i, float(EPS), None, ALU.add)), r=[kmv], w=[kmv])
                self.act(mv[:, 2:3], mv[:, 2:3], AF.Sqrt, r=[kmv], w=[kmv])
                self.sc.op("dve", lambda e, o=mv[:, 2:3]: e.reciprocal(o, o), r=[kmv], w=[kmv])
                self.sc.op("dve", (lambda e, o=xn, i=r_, s1=mv[:, 0:1], s2=mv[:, 2:3]:
                                   e.tensor_scalar(o, i, s1, s2, ALU.subtract, ALU.mult)),
                           r=[kr + "h0", kr + "h1", kmv], w=[kxn])
                self.tt("pool", xn, xn, G, ALU.mult, r=[kxn, kG], w=[kxn])
                self.tt("pool", o_, xn, Bt, ALU.add, r=[kxn, kBt], w=[ko])
                t0 = row0 + s * 128
                dst = out_d[t0:t0 + 128, :]
                self.dma(dst, o_, r=[ko], w=[outkey], dsem=ko)
                it += 1


PAIRS = [[0, 1], [2, 3], [4, 5], [6, 7]]


def p_exchange(pg, src_f32, dst_f32, srckey, dstkey, name):
    pg.sc.op("pool", lambda e: e.collective_compute("AllGather", ALU.bypass, replica_groups=PAIRS,
                                                    ins=[src_f32], outs=[dst_f32]),
             r=[srckey], w=[dstkey], dsem=name, dinc=1)


def build_fused(S):
    pg = Prog(S)
    cons_d = pg.dram("cons", [128, NCONS], F32, "ExternalInput")
    cos_d = pg.dram("cos", [128, S], F32, "ExternalInput")
    sin_d = pg.dram("sin", [128, S], F32, "ExternalInput")
    es_d = pg.dram("es", [128, 16], F32, "ExternalInput")
    x0 = pg.dram("x", [S, D], F32, "ExternalInput")
    xnat = pg.dram("xnat", [S, D], F32, "ExternalInput")
    out_d = pg.dram("out", [S, D], F32, "ExternalOutput")
    pg.load_consts(cons_d)
    xcur, xkey = x0, "D:x0"
    for layer in range(DEPTH):
        nfm, nv, nvirt = layer_dims(layer)
        Fc = nfm * 128 + sum(nv)
        ln = str(layer)
        w_d = pg.dram("w%d" % layer, [D, Fc], F32, "ExternalInput")
        wo_d = pg.dram("wo%d" % layer, [D, D], F32, "ExternalInput")
        g_d = pg.dram("g%d" % layer, [128, D], F32, "ExternalInput")
        b_d = pg.dram("b%d" % layer, [128, D], F32, "ExternalInput")
        fm_d = pg.dram("fm%d" % layer, [nfm * 128, S], BF16)
        vva_ds = [pg.dram("vva%d_%d" % (layer, i), [S, nvirt[i] * 128], BF16) for i in range(len(nv))]
        ygh = pg.dram("ygh%d" % layer, [512, S // 2], F32)
        ygf = pg.dram("ygf%d" % layer, [1024, S // 2], F32)
        yg_bf = ygh.bitcast(BF16)
        ygf_bf = ygf.bitcast(BF16)
        def xch(hp, ygh=ygh, ygf=ygf, ln=ln):
            p_exchange(pg, ygh[hp * 128:(hp + 1) * 128, :], ygf[hp * 256:(hp + 1) * 256, :],
                       "D:yg" + ln, "D:ygf" + ln, "cc" + ln)
        pg.after_hp = xch
        km = pg.p1(layer, xcur, xkey, w_d, cos_d, sin_d, fm_d, vva_ds, ln)
        if layer == 0:
            pg.p2_sb(fm_d, vva_ds[0], yg_bf, ln)
        elif layer == 1:
            pg.p2_moba(fm_d, vva_ds[0], yg_bf, km[0], km[1], ln)
        elif layer == 2:
            pg.p2_swa(fm_d, vva_ds[0], yg_bf, es_d, 0, ln)
        else:
            pg.p2_dil(fm_d, vva_ds, yg_bf, ln)
        pg.after_hp = None
        if layer < DEPTH - 1:
            xn = pg.dram("xs%d" % (layer + 1), [S, D], F32)
            nkey = "D:x%d" % (layer + 1)
        else:
            xn, nkey = out_d, "D:out"
        pg.p3(S, xcur, xkey, [ygf_bf[0:512, :], ygf_bf[512:1024, :]], ["D:ygf" + ln, "D:ygf" + ln],
              wo_d, g_d, b_d, xn, nkey, layer == 0, ln, x_nat=xnat, perm=True)
        xcur, xkey = xn, nkey
    pg.sc.finish()
    pg.sc.emit()
    return pg


def kernel_fused(inputs):
    x = np.asarray(inputs["x"], np.float32)
    B, S, _ = x.shape
    pg = _prog(("F", S), lambda: build_fused(S))
    cons = make_consts()
    cos, sin = rope_tabs(S)
    w_ins = [np.asarray(inputs["sb_w_in"], np.float32), np.asarray(inputs["moba_w_in"], np.float32),
             np.asarray(inputs["swa_w_in"], np.float32), np.asarray(inputs["dil_w_in"], np.float32)]
    w_outs = [np.asarray(inputs["sb_w_out"], np.float32), np.asarray(inputs["moba_w_out"], np.float32),
              np.asarray(inputs["swa_w_out"], np.float32), np.asarray(inputs["dil_w_out"], np.float32)]
    gs = [inputs["ln0_g"], inputs["ln1_g"], inputs["ln2_g"], inputs["ln3_g"]]
    bs = [inputs["ln0_b"], inputs["ln1_b"], inputs["ln2_b"], inputs["ln3_b"]]
    sinks = np.asarray(inputs["swa_sinks"], np.float32)
    rep = lambda v: np.ascontiguousarray(np.broadcast_to(np.asarray(v, np.float32)[None, :], (128, D)))
    in_maps = []
    for b in range(B):
        xr = np.ascontiguousarray(x[b][::-1])
        for hh in range(2):
            m = dict(x=xr, xnat=np.ascontiguousarray(x[b]), cons=cons, cos=cos, sin=sin)
            es = np.zeros((128, 16), np.float32)
            es[:, 0:8] = np.broadcast_to(sinks[8 * hh:8 * hh + 8][None, :], (128, 8))
            m["es"] = es
            for layer in range(DEPTH):
                fm, vg = layer_cols(layer, hh)
                cols = np.concatenate([f[1] for f in fm] + [v[0] for v in vg])
                m["w%d" % layer] = np.ascontiguousarray(w_ins[layer][:, cols])
                m["wo%d" % layer] = w_outs[layer]
                m["g%d" % layer] = rep(gs[layer])
                m["b%d" % layer] = rep(bs[layer])
            in_maps.append(m)
    res = run_bass_kernel_spmd(pg.nc, in_maps, core_ids=list(range(len(in_maps))))
    T = S // 2
    outs = []
    for b in range(B):
        outs.append(np.concatenate([res.results[2 * b]["out"][0:T], res.results[2 * b + 1]["out"][T:S]], axis=0))
    return np.stack(outs, axis=0).astype(np.float32)


def build_A(layer, S, hh_sym=0):
    pg = Prog(S)
    nfm, nv, nvirt = layer_dims(layer)
    Fc = nfm * 128 + sum(nv)
    x_d = pg.dram("x", [S, D], F32, "ExternalInput")
    w_d = pg.dram("w", [D, Fc], F32, "ExternalInput")
    cons_d = pg.dram("cons", [128, NCONS], F32, "ExternalInput")
    cos_d = pg.dram("cos", [128, S], F32, "ExternalInput")
    sin_d = pg.dram("sin", [128, S], F32, "ExternalInput")
    es_d = pg.dram("es", [128, 16], F32, "ExternalInput")
    yg_d = pg.dram("yg", [512, S], BF16, "ExternalOutput")
    fm_d = pg.dram("fm", [nfm * 128, S], BF16)
    vva_ds = [pg.dram("vva%d" % i, [S, nvirt[i] * 128], BF16) for i in range(len(nv))]
    ln = str(layer)
    pg.load_consts(cons_d)
    km = pg.p1(layer, x_d, "D:x", w_d, cos_d, sin_d, fm_d, vva_ds, ln)
    if layer == 0:
        pg.p2_sb(fm_d, vva_ds[0], yg_d, ln)
    elif layer == 1:
        pg.p2_moba(fm_d, vva_ds[0], yg_d, km[0], km[1], ln)
    elif layer == 2:
        pg.p2_swa(fm_d, vva_ds[0], yg_d, es_d, 0, ln)
    else:
        pg.p2_dil(fm_d, vva_ds, yg_d, ln)
    pg.sc.finish()
    pg.sc.emit()
    return pg


def build_B(T):
    pg = Prog(T)
    x_d = pg.dram("x", [T, D], F32, "ExternalInput")
    yga = pg.dram("yga", [512, T], BF16, "ExternalInput")
    ygb = pg.dram("ygb", [512, T], BF16, "ExternalInput")
    wo_d = pg.dram("wo", [D, D], F32, "ExternalInput")
    g_d = pg.dram("g", [128, D], F32, "ExternalInput")
    b_d = pg.dram("b", [128, D], F32, "ExternalInput")
    out_d = pg.dram("out", [T, D], F32, "ExternalOutput")
    pg.p3(T, x_d, "D:x", [yga, ygb], ["D:yga", "D:ygb"], wo_d, g_d, b_d, out_d, "D:out", False, "b")
    pg.sc.finish()
    pg.sc.emit()
    return pg


_CACHE = {}


def _prog(key, fn):
    if key not in _CACHE:
        _CACHE[key] = fn()
    return _CACHE[key]


W_IN = ("sb_w_in", "moba_w_in", "swa_w_in", "dil_w_in")
W_OUT = ("sb_w_out", "moba_w_out", "swa_w_out", "dil_w_out")


def run_layer_A(layer, xs, w_in, sinks, S, cores=None):
    nb = len(xs)
    pg = _prog(("A", layer, S), lambda: build_A(layer, S))
    cons = make_consts()
    cos, sin = rope_tabs(S)
    in_maps = []
    for b in range(nb):
        for hh in range(2):
            fm, vg = layer_cols(layer, hh)
            cols = np.concatenate([f[1] for f in fm] + [v[0] for v in vg])
            es = np.zeros((128, 16), np.float32)
            if sinks is not None:
                es[:, 0:8] = np.broadcast_to(np.asarray(sinks, np.float32)[8 * hh:8 * hh + 8][None, :], (128, 8))
            in_maps.append(dict(x=np.ascontiguousarray(xs[b]), w=np.ascontiguousarray(w_in[:, cols]),
                                cons=cons, cos=cos, sin=sin, es=es))
    res = run_bass_kernel_spmd(pg.nc, in_maps, core_ids=list(range(len(in_maps))))
    out = []
    for b in range(nb):
        out.append(np.concatenate([res.results[2 * b]["yg"], res.results[2 * b + 1]["yg"]], axis=0))
    return out


def run_layer_B(xs, ygs, w_out, g, bvec, S):
    nb = len(xs)
    T = S // 2
    pg = _prog(("B", T), lambda: build_B(T))
    gt = np.ascontiguousarray(np.broadcast_to(np.asarray(g, np.float32)[None, :], (128, D)))
    bt = np.ascontiguousarray(np.broadcast_to(np.asarray(bvec, np.float32)[None, :], (128, D)))
    in_maps = []
    for b in range(nb):
        for hf in range(2):
            sl = slice(hf * T, (hf + 1) * T)
            in_maps.append(dict(x=np.ascontiguousarray(xs[b][sl]),
                                yga=np.ascontiguousarray(ygs[b][0:512, sl]),
                                ygb=np.ascontiguousarray(ygs[b][512:1024, sl]),
                                wo=np.ascontiguousarray(w_out), g=gt, b=bt))
    res = run_bass_kernel_spmd(pg.nc, in_maps, core_ids=list(range(len(in_maps))))
    return [np.concatenate([res.results[2 * b]["out"], res.results[2 * b + 1]["out"]], axis=0) for b in range(nb)]


FUSED = True


def kernel(**inputs):
    if FUSED:
        return kernel_fused(inputs)
    x = np.asarray(inputs["x"], np.float32)
    B, S, _ = x.shape
    xs = [x[b] for b in range(B)]
    for layer in range(DEPTH):
        w_in = np.asarray(inputs[W_IN[layer]], np.float32)
        w_out = np.asarray(inputs[W_OUT[layer]], np.float32)
        g = inputs["ln%d_g" % layer]
        bv = inputs["ln%d_b" % layer]
        sinks = inputs["swa_sinks"] if layer == 2 else None
        if layer == 0:
            xin = [np.ascontiguousarray(a[::-1]) for a in xs]
        else:
            xin = xs
        ygs = run_layer_A(layer, xin, w_in, sinks, S)
        xo = run_layer_B(xin, ygs, w_out, g, bv, S)
        if layer == 0:
            xo = [np.ascontiguousarray(a[::-1]) for a in xo]
        xs = xo
    return np.stack(xs, axis=0).astype(np.float32)
```
